# Optimizing a Trainium2 kernel written in Bass

```python
import math
import jax, jax.numpy as jnp
from jax import lax
import numpy as np

D_MODEL = 1024
BATCH = 32
SEQ = 256
DEPTH = 2
DEC_BATCH = 8
DEC_SEQ = 4096
PAST_LEN = 512

GRID_W = 64
N_EVEN = (DEPTH + 1) // 2
N_ODD = DEPTH // 2
EPS = 1e-6

RET_HEADS = 4
RET_DK = 128
RET_DV = 128
RET_WIDTH = RET_HEADS * RET_DV
RET_CHUNK = 128
RET_BWD_OFFSET = 0.5
CONV_WIDTH = D_MODEL - RET_WIDTH
CONV_K = 3
EVEN_IN = 4 * RET_WIDTH + 4 * CONV_WIDTH
EVEN_MIX = RET_WIDTH + CONV_WIDTH
MLA_HEADS = 8
QK_NOPE = 128
QK_ROPE = 64
V_HEAD = 128
Q_LORA = 384
KV_LORA = 256
MLA_WIDTH = MLA_HEADS * V_HEAD
ODD_IN = Q_LORA + KV_LORA + QK_ROPE + MLA_WIDTH
ROPE_BASE = 10000.0
Q_BLOCK = 128

kernel_name = "hybrid_retconv_mla_diffusion_step"

F32 = jnp.float32


def rms(x, g):
    xf = x.astype(F32)
    y = xf * lax.rsqrt(jnp.mean(xf * xf, axis=-1, keepdims=True) + EPS)
    return (y * g.astype(F32)).astype(x.dtype)


def ada(cvec, w, b):
    m = jax.nn.silu(cvec) @ w + b
    sh, sc, g = jnp.split(m, 3, axis=-1)
    return sh[:, None, :], sc[:, None, :], g[:, None, :]


def ret_log_decay(offset):
    h = jnp.arange(RET_HEADS, dtype=F32)
    return jnp.log(1.0 - 2.0 ** (-5.0 - h - offset))


def retention_scan(q, k, v, log_g, s0):
    B, T, H, _ = q.shape
    C = RET_CHUNK
    N = T // C
    idx = jnp.arange(C, dtype=F32)
    diff = idx[:, None] - idx[None, :]
    mask = jnp.where(diff >= 0, jnp.exp(log_g[:, None, None] * jnp.maximum(diff, 0.0)), 0.0)
    q_dec = jnp.exp(log_g[None, :] * (idx[:, None] + 1.0))
    k_dec = jnp.exp(log_g[None, :] * (C - 1.0 - idx[:, None]))
    c_dec = jnp.exp(log_g * C)

    def chunks(a):
        return a.reshape(B, N, C, H, a.shape[-1]).transpose(1, 0, 2, 3, 4)

    def step(s, qkv):
        qc, kc, vc = qkv
        att = jnp.einsum('bihd,bjhd->bhij', qc, kc) * mask
        o = (jnp.einsum('bhij,bjhe->bihe', att, vc)
             + jnp.einsum('bihd,bhde->bihe', qc, s) * q_dec[None, :, :, None])
        s = s * c_dec[None, :, None, None] + jnp.einsum('bjhd,bjhe->bhde', kc * k_dec[None, :, :, None], vc)
        return s, o

    s_fin, o = lax.scan(step, s0, (chunks(q), chunks(k), chunks(v)))
    return o.transpose(1, 0, 2, 3, 4).reshape(B, T, H, v.shape[-1]), s_fin


def ret_conv_mixer(h, w_in, conv_w, w_out, s_fwd0, s_bwd0):
    B, T, _ = h.shape
    R, Cw = RET_WIDTH, CONV_WIDTH
    p = h @ w_in
    q, k, v, g_a, bg, cg, xh, g_b = jnp.split(
        p, [R, 2 * R, 3 * R, 4 * R, 4 * R + Cw, 4 * R + 2 * Cw, 4 * R + 3 * Cw], axis=-1)
    q = q.reshape(B, T, RET_HEADS, RET_DK).astype(F32)
    k = k.reshape(B, T, RET_HEADS, RET_DK).astype(F32) * (RET_DK ** -0.5)
    v = v.reshape(B, T, RET_HEADS, RET_DV).astype(F32)
    o_f, s_f = retention_scan(q, k, v, ret_log_decay(0.0), s_fwd0)
    o_b, s_b = retention_scan(q[:, ::-1], k[:, ::-1], v[:, ::-1], ret_log_decay(RET_BWD_OFFSET), s_bwd0)
    o = o_f + o_b[:, ::-1]
    o = o * lax.rsqrt(jnp.mean(o * o, axis=-1, keepdims=True) + EPS)
    y_a = o.reshape(B, T, R).astype(h.dtype) * jax.nn.silu(g_a)
    z = cg * xh
    zp = jnp.pad(z, ((0, 0), (1, 1), (0, 0)))
    zc = zp[:, :-2] * conv_w[0] + zp[:, 1:-1] * conv_w[1] + zp[:, 2:] * conv_w[2]
    y_b = jax.nn.silu(g_b) * bg * zc
    return jnp.concatenate([y_a, y_b], axis=-1) @ w_out, s_f, s_b


def _rot(x, ang):
    f = ang.shape[-1]
    cos = jnp.cos(ang)[:, None, :]
    sin = jnp.sin(ang)[:, None, :]
    x1, x2 = x[..., :f], x[..., f:]
    return jnp.concatenate([x1 * cos - x2 * sin, x1 * sin + x2 * cos], axis=-1)


def rope_2d(x):
    T = x.shape[1]
    rows = T // GRID_W
    row = jnp.repeat(jnp.arange(rows, dtype=F32), GRID_W)
    col = jnp.tile(jnp.arange(GRID_W, dtype=F32), rows)
    f = QK_ROPE // 4
    inv = ROPE_BASE ** (-jnp.arange(f, dtype=F32) / f)
    xf = x.astype(F32)
    half = QK_ROPE // 2
    out = jnp.concatenate([_rot(xf[..., :half], row[:, None] * inv),
                           _rot(xf[..., half:], col[:, None] * inv)], axis=-1)
    return out.astype(x.dtype)


def mla_project(h, w_in, qn_g, kvn_g, q_up):
    B, T, _ = h.shape
    q_lat, kv_lat, k_rope, gate = jnp.split(h @ w_in, [Q_LORA, Q_LORA + KV_LORA, Q_LORA + KV_LORA + QK_ROPE], axis=-1)
    q = (rms(q_lat, qn_g) @ q_up).reshape(B, T, MLA_HEADS, QK_NOPE + QK_ROPE)
    ckv = rms(kv_lat, kvn_g)
    return q[..., :QK_NOPE], q[..., QK_NOPE:], ckv, k_rope, gate


def mla_expand(ckv, kv_up):
    B, L, _ = ckv.shape
    kv = (ckv @ kv_up).reshape(B, L, MLA_HEADS, QK_NOPE + V_HEAD)
    return kv[..., :QK_NOPE], kv[..., QK_NOPE:]


def mla_attend(q_nope, q_rope, k_nope, k_rope, v):
    B, T, H, _ = q_nope.shape
    NB = T // Q_BLOCK
    scale = (QK_NOPE + QK_ROPE) ** -0.5

    def blocks(a):
        return a.reshape(B, NB, Q_BLOCK, H, a.shape[-1]).transpose(1, 0, 2, 3, 4)

    def one(qs):
        qn, qr = qs
        s = jnp.einsum('bqhd,bkhd->bhqk', qn, k_nope) + jnp.einsum('bqhr,bkr->bhqk', qr, k_rope)
        pr = jax.nn.softmax(s.astype(F32) * scale, axis=-1).astype(v.dtype)
        return jnp.einsum('bhqk,bkhe->bqhe', pr, v)

    o = lax.map(one, (blocks(q_nope), blocks(q_rope)))
    return o.transpose(1, 0, 2, 3, 4).reshape(B, T, H * V_HEAD)


def mla_context(h, w_in, qn_g, kvn_g, q_up, kv_up, w_out):
    q_nope, q_rope, ckv, k_rope, gate = mla_project(h, w_in, qn_g, kvn_g, q_up)
    k_nope, v = mla_expand(ckv, kv_up)
    o = mla_attend(q_nope, q_rope, k_nope, k_rope, v)
    return (o * jax.nn.silu(gate)) @ w_out, ckv, k_rope


def mla_latent(h, w_in, qn_g, kvn_g, q_up, kv_up, w_out, ctx_ckv, ctx_krope):
    q_nope, q_rope, ckv, k_rope, gate = mla_project(h, w_in, qn_g, kvn_g, q_up)
    q_rope = rope_2d(q_rope)
    k_rope = rope_2d(k_rope[:, :, None, :])[:, :, 0, :]
    k_nope_l, v_l = mla_expand(ckv, kv_up)
    k_nope_c, v_c = mla_expand(ctx_ckv, kv_up)
    k_nope = jnp.concatenate([k_nope_c, k_nope_l], axis=1)
    k_r = jnp.concatenate([ctx_krope, k_rope], axis=1)
    v = jnp.concatenate([v_c, v_l], axis=1)
    o = mla_attend(q_nope, q_rope, k_nope, k_r, v)
    return (o * jax.nn.silu(gate)) @ w_out


def setup_inputs(seed: int = 0) -> dict:
    key = jax.random.key(seed)
    ks = jax.random.split(key, 24)

    def nrm(k, shape, scale):
        return jax.random.normal(k, shape, F32) * scale

    return {
        "x_prompt": nrm(ks[0], (BATCH, SEQ, D_MODEL), 1.0),
        "x_sample": nrm(ks[1], (DEC_BATCH, DEC_SEQ, D_MODEL), 1.0),
        "c": nrm(ks[2], (DEC_BATCH, D_MODEL), 1.0),
        "state_ret_fwd": nrm(ks[3], (DEC_BATCH, N_EVEN, RET_HEADS, RET_DK, RET_DV), 0.5),
        "state_ret_bwd": nrm(ks[4], (DEC_BATCH, N_EVEN, RET_HEADS, RET_DK, RET_DV), 0.5),
        "cache_mla_ckv": nrm(ks[5], (DEC_BATCH, N_ODD, PAST_LEN, KV_LORA), 1.0),
        "cache_mla_krope": nrm(ks[6], (DEC_BATCH, N_ODD, PAST_LEN, QK_ROPE), 1.0),
        "c_ctx": nrm(ks[7], (D_MODEL,), 1.0),
        "ada_w": nrm(ks[8], (DEPTH, D_MODEL, 3 * D_MODEL), 0.5 * D_MODEL ** -0.5),
        "ada_b": nrm(ks[9], (DEPTH, 3 * D_MODEL), 0.02),
        "norm_g": 1.0 + nrm(ks[10], (DEPTH, D_MODEL), 0.02),
        "even_in_w": nrm(ks[11], (N_EVEN, D_MODEL, EVEN_IN), D_MODEL ** -0.5),
        "even_conv_w": nrm(ks[12], (N_EVEN, CONV_K, CONV_WIDTH), CONV_K ** -0.5),
        "even_out_w": nrm(ks[13], (N_EVEN, EVEN_MIX, D_MODEL), EVEN_MIX ** -0.5),
        "odd_in_w": nrm(ks[14], (N_ODD, D_MODEL, ODD_IN), D_MODEL ** -0.5),
        "odd_q_norm_g": 1.0 + nrm(ks[15], (N_ODD, Q_LORA), 0.02),
        "odd_kv_norm_g": 1.0 + nrm(ks[16], (N_ODD, KV_LORA), 0.02),
        "odd_q_up_w": nrm(ks[17], (N_ODD, Q_LORA, MLA_HEADS * (QK_NOPE + QK_ROPE)), Q_LORA ** -0.5),
        "odd_kv_up_w": nrm(ks[18], (N_ODD, KV_LORA, MLA_HEADS * (QK_NOPE + V_HEAD)), KV_LORA ** -0.5),
        "odd_out_w": nrm(ks[19], (N_ODD, MLA_WIDTH, D_MODEL), MLA_WIDTH ** -0.5),
        "final_norm_g": 1.0 + nrm(ks[20], (D_MODEL,), 0.02),
    }


def reference(x_prompt, x_sample, c, state_ret_fwd, state_ret_bwd, cache_mla_ckv, cache_mla_krope, c_ctx,
              ada_w, ada_b, norm_g, even_in_w, even_conv_w, even_out_w, odd_in_w, odd_q_norm_g,
              odd_kv_norm_g, odd_q_up_w, odd_kv_up_w, odd_out_w, final_norm_g):
    xp, xs = x_prompt, x_sample
    B = xp.shape[0]
    new_sf, new_sb, new_ckv, new_kr = [], [], [], []
    for l in range(DEPTH):
        sh_p, sc_p, g_p = ada(c_ctx[None, :], ada_w[l], ada_b[l])
        sh_s, sc_s, g_s = ada(c, ada_w[l], ada_b[l])
        hp = rms(xp, norm_g[l]) * (1.0 + sc_p) + sh_p
        hs = rms(xs, norm_g[l]) * (1.0 + sc_s) + sh_s
        i = l // 2
        if l % 2 == 0:
            zero = jnp.zeros((B, RET_HEADS, RET_DK, RET_DV), F32)
            yp, sf, sb = ret_conv_mixer(hp, even_in_w[i], even_conv_w[i], even_out_w[i], zero, zero)
            ys, _, _ = ret_conv_mixer(hs, even_in_w[i], even_conv_w[i], even_out_w[i],
                                      state_ret_fwd[:, i].astype(F32), state_ret_bwd[:, i].astype(F32))
            new_sf.append(sf.astype(xp.dtype))
            new_sb.append(sb.astype(xp.dtype))
        else:
            yp, ckv, kr = mla_context(hp, odd_in_w[i], odd_q_norm_g[i], odd_kv_norm_g[i],
                                      odd_q_up_w[i], odd_kv_up_w[i], odd_out_w[i])
            ys = mla_latent(hs, odd_in_w[i], odd_q_norm_g[i], odd_kv_norm_g[i], odd_q_up_w[i],
                            odd_kv_up_w[i], odd_out_w[i], cache_mla_ckv[:, i], cache_mla_krope[:, i])
            new_ckv.append(ckv)
            new_kr.append(kr)
        xp = xp + g_p * yp
        xs = xs + g_s * ys
    y_prompt = rms(xp, final_norm_g)
    y_sample = rms(xs, final_norm_g)
    return (y_prompt, y_sample, jnp.stack(new_sf, axis=1), jnp.stack(new_sb, axis=1),
            jnp.stack(new_ckv, axis=1), jnp.stack(new_kr, axis=1))
```

```python
import contextlib
import numpy as np
import concourse.bass as bass
import concourse.mybir as mybir
from concourse.bass_utils import run_bass_kernel_spmd

F32 = mybir.dt.float32
BF16 = mybir.dt.bfloat16
AF = mybir.ActivationFunctionType
ALU = mybir.AluOpType

ENG_NAMES = ("pe", "act", "dve", "pool", "sp")
DMA_RING = 12
EPS = 1e-6

N_CORES = 8
D = 1024
T_S = 4096
T_P = 256
N_P = 4
ROWS = T_S + N_P * T_P
NCH = ROWS // 128
PAST = 512


class T:
    __slots__ = ("name", "w", "r", "excl")

    def __init__(self, name="", excl=False):
        self.name = name
        self.w = None
        self.r = {}
        self.excl = excl


class Sched:
    def __init__(self, nc, stack):
        self.nc = nc
        self.sem = {e: stack.enter_context(nc.semaphore("s_" + e)) for e in ENG_NAMES}
        self.cnt = {e: 0 for e in ENG_NAMES}
        self.prog = {e: [] for e in ENG_NAMES}
        self.seen = {e: {} for e in ENG_NAMES}
        self.dsem = {}
        self.dcnt = {}
        for q in ("sp", "pool", "act"):
            self.dsem[q] = [stack.enter_context(nc.semaphore("d_%s%d" % (q, i)))
                            for i in range(DMA_RING)]
            self.dcnt[q] = 0
        self.out_events = []

    def _deps(self, eng, reads, writes):
        best = {}

        def add(key, v):
            if v > best.get(key, 0):
                best[key] = v

        for t in reads:
            w = t.w
            if w is not None and not (w[0] == ("e", "pe") and eng == "pe"):
                add(w[0], w[1])
            if t.excl:
                for key, v in t.r.items():
                    if key != ("e", eng):
                        add(key, v)
        for t in writes:
            w = t.w
            if w is not None and w[0] != ("e", eng):
                add(w[0], w[1])
            for key, v in t.r.items():
                if key != ("e", eng):
                    add(key, v)
        waits = []
        seen = self.seen[eng]
        for key, v in best.items():
            if seen.get(key, 0) >= v:
                continue
            seen[key] = v
            waits.append((key, v))
        return waits

    def _semof(self, key):
        if key[0] == "e":
            return self.sem[key[1]]
        q, i = key[1]
        return self.dsem[q][i]

    def _mark(self, key, val, reads, writes):
        for t in reads:
            if t.r.get(key, 0) < val:
                t.r[key] = val
        for t in writes:
            t.w = (key, val)
            t.r = {}

    def op(self, eng, fn, reads=(), writes=()):
        waits = self._deps(eng, reads, writes)
        self.cnt[eng] += 1
        self.prog[eng].append(("op", waits, fn, None))
        self._mark(("e", eng), self.cnt[eng], reads, writes)

    def dma(self, q, out, in_, reads=(), writes=(), is_output=False, **kw):
        i = self.dcnt[q] % DMA_RING
        gen = self.dcnt[q] // DMA_RING
        self.dcnt[q] += 1
        waits = self._deps(q, reads, writes)
        key = ("d", (q, i))
        if gen > 0:
            prev = 16 * gen
            if self.seen[q].get(key, 0) < prev:
                self.seen[q][key] = prev
                waits.append((key, prev))
        val = 16 * (gen + 1)
        self.prog[q].append(("dma", waits, (out, in_, kw), self.dsem[q][i]))
        self._mark(key, val, reads, writes)
        if is_output:
            self.out_events.append((key, val))

    def barrier(self):
        tgt = []
        for e in ENG_NAMES:
            if self.cnt[e] > 0:
                tgt.append((("e", e), self.cnt[e]))
        for q in self.dsem:
            n = self.dcnt[q]
            for i in range(DMA_RING):
                k = (n - i + DMA_RING - 1) // DMA_RING
                if k > 0:
                    tgt.append((("d", (q, i)), 16 * k))
        for e in ENG_NAMES:
            waits = []
            for key, v in tgt:
                if key == ("e", e) and e != "sp":
                    pass
                if self.seen[e].get(key, 0) < v:
                    self.seen[e][key] = v
                    waits.append((key, v))
            self.prog[e].append(("waitonly", waits, None, None))

    def finish(self):
        best = {}
        for key, v in self.out_events:
            best[key] = max(best.get(key, 0), v)
        self.prog["sp"].append(("waitonly", list(best.items()), None, None))

    def emit(self, block):
        S = self
        nc = self.nc
        engs = {"pe": nc.tensor, "act": nc.scalar, "dve": nc.vector, "pool": nc.gpsimd, "sp": nc.sync}

        def run(ename, engobj):
            own = S.sem[ename]
            for kind, waits, payload, inc in S.prog[ename]:
                for key, v in waits:
                    engobj.wait_ge(S._semof(key), v)
                if kind == "op":
                    payload(engobj).then_inc(own, 1)
                elif kind == "dma":
                    out, in_, kw = payload
                    engobj.dma_start(out=out, in_=in_, **kw).then_inc(inc, 16)

        @block.tensor
        def _(e):
            run("pe", e)

        @block.scalar
        def _(e):
            run("act", e)

        @block.vector
        def _(e):
            run("dve", e)

        @block.gpsimd
        def _(e):
            run("pool", e)

        @block.sync
        def _(e):
            run("sp", e)


class Ring:
    def __init__(self, bufs, excl=False):
        self.bufs = bufs
        self.ts = [T(excl=excl) for _ in bufs]
        self.i = -1

    def next(self):
        self.i = (self.i + 1) % len(self.bufs)
        return self.bufs[self.i], self.ts[self.i]

    def cur(self):
        return self.bufs[self.i], self.ts[self.i]

    def sub(self, idx):
        r = Ring([self.bufs[i] for i in idx])
        r.ts = [self.ts[i] for i in idx]
        return r


def l0_tables():
    C = 128
    scale = 128.0 ** -0.5
    tab = np.zeros((128, 5, 4, 128), np.float64)
    cF, cB = [], []
    i = np.arange(C, dtype=np.float64)
    for h in range(4):
        gf = 1.0 - 2.0 ** (-5.0 - h)
        gb = 1.0 - 2.0 ** (-5.5 - h)
        cF.append(float(np.float32(gf ** C)))
        cB.append(float(np.float32(gb ** C)))
        ii = i[None, :]
        jj = i[:, None]
        m = np.where(ii > jj, gf ** np.maximum(ii - jj, 0), np.where(jj > ii, gb ** np.maximum(jj - ii, 0), 2.0))
        tab[:, 0, h, :] = scale * m
        tab[:, 1, h, :] = (gf ** (i + 1.0))[None, :]
        tab[:, 2, h, :] = (gb ** (C - i))[None, :]
        tab[:, 3, h, :] = (scale * gf ** (C - 1.0 - i))[:, None]
        tab[:, 4, h, :] = (scale * gb ** i)[:, None]
    return tab.reshape(128, 5, 512).astype(np.float32), cF, cB


def rope_table():
    f = 16
    inv = (np.float32(10000.0) ** (-np.arange(f, dtype=np.float32) / np.float32(f))).astype(np.float32)
    idx = np.arange(64, dtype=np.float32)
    tab = np.zeros((64, 2, 64), np.float32)
    for p in range(64):
        ang = (idx * inv[p % 16]).astype(np.float32)
        sign = -1.0 if (p % 32) < 16 else 1.0
        tab[p, 0] = np.cos(ang)
        tab[p, 1] = sign * np.sin(ang)
    return tab


def build_program(dbg=False, do_l1=True, run_sample=True, n_prompts=N_P, do_l0=True):
    nc = bass.Bass("TRN2", target_bir_lowering=False)
    _, cF, cB = l0_tables()

    def din(name, shape, dt=F32):
        return nc.dram_tensor(name, list(shape), dt, kind="ExternalInput").ap()

    def dout(name, shape, dt=F32):
        return nc.dram_tensor(name, list(shape), dt, kind="ExternalOutput").ap()

    def dscr(name, shape, dt=F32):
        return nc.dram_tensor(name, list(shape), dt, kind="Internal").ap()

    x_all = din("x_all", [ROWS, D])
    cvec = din("cvec", [128, 8, 2])
    st_f = din("st_f", [4, 128, 128])
    st_b = din("st_b", [4, 128, 128])
    ada_w = din("ada_w", [2, D, 3 * D])
    ada_b_pp = din("ada_b_pp", [128, 2, 24])
    ada_b_row = din("ada_b_row", [2, 3 * D])
    ng_pp = din("ng_pp", [128, 2, 8])
    w_in0 = din("w_in0", [D, 4096])
    convw_pp = din("convw_pp", [128, 4, 3])
    w_out0 = din("w_out0", [D, D])
    ident_d = din("ident", [128, 128])
    l0tab_d = din("l0tab", [128, 5, 512])

    w_in1e = din("w_in1e", [D, 1792])
    q_up_e = din("q_up_e", [384, 2048])
    kv_up_d = din("kv_up", [256, 2048])
    w_out1_d = din("w_out1", [D, D])
    qng_pp = din("qng_pp", [128, 3])
    kvng_pp = din("kvng_pp", [128, 2])
    kvng_row = din("kvng_row", [1, 256])
    fng_row = din("fng_row", [1, D])
    ropetab = din("ropetab", [64, 2, 64])
    c_ckv = din("c_ckv", [PAST, 256])
    c_kr = din("c_kr", [PAST, 64])
    gsc = dscr("gsc", [8, 128, ROWS], BF16)
    osc = dscr("osc", [8, 128, ROWS], BF16)
    new_ckv = dout("new_ckv", [N_P, T_P, 256])
    new_kr = dout("new_kr", [N_P, T_P, 64])

    y_all = dout("y_all", [ROWS, D])
    new_sf = dout("new_sf", [N_P, 4, 128, 128])
    new_sb = dout("new_sb", [N_P, 4, 128, 128])
    if dbg:
        x1_hbm = dout("x1_dbg", [ROWS, D])
    else:
        x1_hbm = dscr("x1_scr", [ROWS, D])
    snap_hbm = dscr("snap_scr", [NCH, 128, 512], BF16)
    g_hbm = dscr("g_scr", [2, 2, D])

    with contextlib.ExitStack() as st:
        S = Sched(nc, st)

        uid = [0]

        def sb(stack, name, shape, dt):
            uid[0] += 1
            return stack.enter_context(nc.sbuf_tensor("sb_%s_%d" % (name, uid[0]), list(shape), dt))

        def ps(stack, name, shape, dt):
            return stack.enter_context(nc.psum_tensor("ps_" + name, list(shape), dt))

        pT = Ring([ps(st, "pT%d" % i, [128, 8, 128], BF16) for i in range(2)], excl=True)
        pB = Ring([ps(st, "pB%d" % i, [128, 512], F32) for i in range(6)], excl=True)

        ident_bf = sb(st, "ident_bf", [128, 128], BF16)
        ones_bf = sb(st, "ones_bf", [128, 128], BF16)
        ones_f32 = sb(st, "ones_f32", [128, 128], F32)
        t_const = T("const")
        A_pp = sb(st, "A_pp", [128, 2, 2, 8], F32)
        B_pp = sb(st, "B_pp", [128, 2, 2, 8], F32)
        t_AB = T("AB")
        ssq0 = sb(st, "ssq0", [128, NCH], F32)
        rstd0 = sb(st, "rstd0", [128, NCH], F32)
        ssq1 = sb(st, "ssq1", [128, NCH], F32)
        rstd1 = sb(st, "rstd1", [128, NCH], F32)
        t_r0 = [T() for _ in range(NCH)]
        t_r1 = [T() for _ in range(NCH)]
        t_x1 = [T() for _ in range(NCH)]
        t_snap = [T() for _ in range(NCH)]
        t_g = T("g_hbm")

        eps_t = sb(st, "eps_t", [128, 1], F32)
        one_t = sb(st, "one_t", [128, 1], F32)
        t_eps = T()
        S.op("pool", lambda e: e.memset(eps_t[:], EPS), writes=[t_eps])
        S.op("pool", lambda e: e.memset(one_t[:], 1.0), writes=[t_eps])
        S.dma("pool", ident_bf[:], ident_d, writes=[t_const])
        S.op("pool", lambda e: e.memset(ones_bf[:], 1.0), writes=[t_const])
        S.op("pool", lambda e: e.memset(ones_f32[:], 1.0), writes=[t_const])

        sw = contextlib.ExitStack()
        w_in = sb(sw, "w_in", [128, 8, 4096], BF16)
        w_out = sb(sw, "w_out", [128, 8, 1024], BF16)
        t_win, t_wout = T("w_in"), T("w_out")
        t_wkv = T("w_in_kv")
        if do_l0:
            for cb in (1, 2, 0, 3, 4, 5, 6, 7):
                S.dma("pool", w_in[:, :, cb * 512:(cb + 1) * 512],
                      w_in0[:, cb * 512:(cb + 1) * 512].rearrange("(c p) n -> p c n", p=128), writes=[t_wkv if cb in (1, 2) else t_win])
            for cb in range(2):
                S.dma("pool", w_out[:, :, cb * 512:(cb + 1) * 512],
                      w_out0[:, cb * 512:(cb + 1) * 512].rearrange("(c p) n -> p c n", p=128), writes=[t_wout])
        with contextlib.ExitStack() as sa:
            cv = sb(sa, "cv", [128, 16], F32)
            cve = sb(sa, "cve", [128, 16], F32)
            sc_bf = sb(sa, "sc_bf", [128, 8, 2], BF16)
            adab = sb(sa, "adab", [128, 2, 24], F32)
            ngp = sb(sa, "ngp", [128, 2, 8], F32)
            brow = sb(sa, "brow", [2, 2, 3 * D], F32)
            grow = sb(sa, "grow", [2, 2, D], F32)
            mpp = sb(sa, "mpp", [128, 2, 16, 2], F32)
            aw32 = Ring([sb(sa, "aw32_%d" % i, [128, 8, 512], F32) for i in range(2)])
            aw = Ring([sb(sa, "aw%d" % i, [128, 8, 512], BF16) for i in range(2)])
            t_cv, t_sc, t_misc, t_grow, t_mpp = T(), T(), T(), T(), T()
            S.dma("sp", cv[:], cvec.rearrange("p c g -> p (c g)"), writes=[t_cv])
            S.dma("sp", adab[:], ada_b_pp, writes=[t_misc])
            S.dma("sp", ngp[:], ng_pp, writes=[t_misc])
            S.dma("sp", brow[0:1, :, :], ada_b_row.rearrange("(o l) n -> o l n", o=1), writes=[t_misc])
            S.dma("sp", brow[1:2, :, :], ada_b_row.rearrange("(o l) n -> o l n", o=1), writes=[t_misc])
            S.op("act", lambda e: e.activation(out=cve[:], in_=cv[:], func=AF.Exp, scale=-1.0), reads=[t_cv], writes=[t_sc])
            S.op("dve", lambda e: e.tensor_scalar(out=cve[:], in0=cve[:], scalar1=1.0, scalar2=None, op0=ALU.add), reads=[t_sc], writes=[t_sc])
            S.op("dve", lambda e: e.reciprocal(out=cve[:], in_=cve[:]), reads=[t_sc], writes=[t_sc])
            S.op("dve", lambda e: e.tensor_tensor(out=sc_bf[:].rearrange("p c g -> p (c g)"), in0=cv[:], in1=cve[:], op=ALU.mult),
                 reads=[t_sc, t_cv], writes=[t_sc])
            for l in range(2):
                for cb in range(6):
                    a32, t_a32 = aw32.next()
                    S.dma("sp", a32[:], ada_w[l, :, cb * 512:(cb + 1) * 512].rearrange("(c p) n -> p c n", p=128), writes=[t_a32])
                    awt, t_aw = aw.next()
                    S.op("act", lambda e, a32=a32, awt=awt: e.copy(out=awt[:, 0:4, :], in_=a32[:, 0:4, :]), reads=[t_a32], writes=[t_aw])
                    S.op("dve", lambda e, a32=a32, awt=awt: e.tensor_copy(out=awt[:, 4:8, :], in_=a32[:, 4:8, :]), reads=[t_a32], writes=[t_aw])
                    if cb < 4:
                        bank, t_bank = pB.next()
                        for j in range(4):
                            for fc in range(8):
                                S.op("pe", lambda e, j=j, fc=fc, bank=bank, awt=awt: e.matmul(
                                    out=bank[:, j * 2:j * 2 + 2], lhsT=awt[:, fc, j * 128:(j + 1) * 128], rhs=sc_bf[:, fc, :],
                                    start=(fc == 0), stop=(fc == 7)), reads=[t_aw, t_sc], writes=[t_bank])
                        S.op("dve", lambda e, bank=bank, l=l, cb=cb: e.tensor_copy(
                            out=mpp[:, l, cb * 4:(cb + 1) * 4, :], in_=bank[:, 0:8].rearrange("p (j g) -> p j g", g=2)),
                            reads=[t_bank], writes=[t_mpp])
                    else:
                        bank, t_bank = pB.next()
                        for fc in range(8):
                            S.op("pe", lambda e, fc=fc, bank=bank, awt=awt: e.matmul(
                                out=bank[0:2, :], lhsT=sc_bf[:, fc, :], rhs=awt[:, fc, :], start=(fc == 0), stop=(fc == 7)),
                                reads=[t_aw, t_sc], writes=[t_bank])
                        S.op("dve", lambda e, bank=bank, l=l, cb=cb: e.tensor_tensor(
                            out=grow[:, l, (cb - 4) * 512:(cb - 3) * 512], in0=bank[0:2, :],
                            in1=brow[:, l, cb * 512:(cb + 1) * 512], op=ALU.add), reads=[t_bank, t_misc], writes=[t_grow])
            for l in range(2):
                for g in range(2):
                    S.op("dve", lambda e, l=l, g=g: e.tensor_tensor(out=B_pp[:, l, g, :], in0=mpp[:, l, 0:8, g], in1=adab[:, l, 0:8], op=ALU.add),
                         reads=[t_mpp, t_misc], writes=[t_AB])
                    S.op("dve", lambda e, l=l, g=g: e.tensor_tensor(out=A_pp[:, l, g, :], in0=mpp[:, l, 8:16, g], in1=adab[:, l, 8:16], op=ALU.add),
                         reads=[t_mpp, t_misc], writes=[t_AB])
                    S.op("dve", lambda e, l=l, g=g: e.scalar_tensor_tensor(out=A_pp[:, l, g, :], in0=A_pp[:, l, g, :], scalar=1.0, in1=ngp[:, l, :],
                                                                          op0=ALU.add, op1=ALU.mult), reads=[t_AB, t_misc], writes=[t_AB])
            S.dma("pool", g_hbm.rearrange("l g n -> g l n"), grow[:], reads=[t_grow], writes=[t_g])
            S.barrier()

        with contextlib.ExitStack() as s0:
          if do_l0:
              l0tab = sb(s0, "l0tab", [128, 5, 512], F32)
              cw = sb(s0, "cw", [128, 4, 3], F32)
              t_tab = T("l0tab")
              S.dma("sp", l0tab[:], l0tab_d, writes=[t_tab])
              S.dma("sp", cw[:], convw_pp, writes=[t_tab])
              maskT, qdecF, qdecB, kdecF, kdecB = (l0tab[:, i, :] for i in range(5))
              G_bc = Ring([sb(s0, "G_bc%d" % i, [128, 1024], F32) for i in range(2)])

              def ring(name, n, shape, dt):
                  return Ring([sb(s0, "%s%d" % (name, i), shape, dt) for i in range(n)])

              xt_r = ring("xt", 4, [128, 1024], F32)
              junk = sb(s0, "junk", [128, 1024], BF16)
              t_junk = T()
              lnt = sb(s0, "lnt", [128, NCH], F32)
              xn_r = ring("xn", 3, [128, 1024], BF16)
              hT_r = ring("hT", 4, [128, 8, 128], BF16)
              qT_r = ring("qT", 2, [128, 512], BF16)
              qTf_r = ring("qTf", 2, [128, 512], BF16)
              qTb_r = ring("qTb", 2, [128, 512], BF16)
              kT_r = ring("kT", 2, [128, 512], BF16)
              sga_r = ring("sga", 2, [128, 512], F32)
              sgb_r = ring("sgb", 1, [128, 512], F32)
              et_r = ring("et", 2, [128, 512], F32)
              cgs_r = ring("cgs", 1, [128, 512], F32)
              u_r = ring("u", 2, [128, 512], F32)
              z_r = ring("z", 2, [128, 4, 130], F32)
              kf_r = ring("kf", 2, [128, 512], BF16)
              vb_r = ring("vb", 2, [128, 512], BF16)
              attm_r = ring("attm", 2, [128, 512], BF16)
              snapr_r = ring("snapr", 2, [128, 512], BF16)
              snapw_r = ring("snapw", 2, [128, 512], BF16)
              osb_r = ring("osb", 1, [128, 512], F32)
              osq_r = ring("osq", 1, [128, 512], BF16)
              ms_r = ring("ms", 1, [128, 512], F32)
              yat_r = ring("yat", 1, [128, 512], F32)
              ymix_r = ring("ymix", 2, [128, 8, 128], BF16)
              zc_r = ring("zc", 1, [128, 4, 128], F32)
              xres_r = ring("xres", 2, [128, 1024], F32)
              x1t_r = ring("x1t", 2, [128, 1024], F32)
              S_f = sb(s0, "S_f", [128, 512], F32)
              S_b = sb(s0, "S_b", [128, 512], F32)
              Sf_bf = sb(s0, "Sf_bf", [128, 512], BF16)
              t_Sf, t_Sb, t_Sfbf = T(), T(), T()

              def silu_psum(bank, t_bank, out_ap, t_out):
                  et, t_et = et_r.next()
                  S.op("act", lambda e: e.activation(out=et[:], in_=bank[:], func=AF.Exp, scale=-1.0), reads=[t_bank], writes=[t_et])
                  S.op("act", lambda e: e.activation(out=et[:], in_=et[:], func=AF.Ln, bias=one_t[:]), reads=[t_et, t_eps], writes=[t_et])
                  S.op("act", lambda e: e.activation(out=et[:], in_=et[:], func=AF.Exp, scale=-1.0), reads=[t_et], writes=[t_et])
                  S.op("dve", lambda e: e.tensor_tensor(out=out_ap, in0=bank[:], in1=et[:], op=ALU.mult), reads=[t_bank, t_et], writes=[t_out])

              def rstd_from_ssq(ssq_col, ln_col, out_col, n, t_col):
                  S.op("act", lambda e: e.activation(out=ln_col, in_=ssq_col, func=AF.Ln, scale=1.0 / n, bias=eps_t[:]),
                       reads=[t_col, t_eps], writes=[t_col])
                  S.op("act", lambda e: e.activation(out=out_col, in_=ln_col, func=AF.Exp, scale=-0.5), reads=[t_col], writes=[t_col])


              def fe_load(gci, stats):
                  xt, t_xt = xt_r.next()
                  S.dma("sp", xt[:], x_all[gci * 128:(gci + 1) * 128, :], writes=[t_xt])
                  if stats:
                      S.op("act", lambda e: e.activation(out=junk[:], in_=xt[:], func=AF.Square, accum_out=ssq0[:, gci:gci + 1]),
                           reads=[t_xt], writes=[t_junk, t_r0[gci]])
                      rstd_from_ssq(ssq0[:, gci:gci + 1], lnt[:, gci:gci + 1], rstd0[:, gci:gci + 1], float(D), t_r0[gci])
                  return xt, t_xt

              def frontend(gci, grp, stats):
                  return fe_main(gci, grp, *fe_load(gci, stats))

              def fe_main(gci, grp, xt, t_xt):
                  return fe_T(gci, grp, *fe_norm(gci, xt, t_xt))

              def fe_norm(gci, xt, t_xt):
                  xn, t_xn = xn_r.next()
                  S.op("dve", lambda e: e.tensor_scalar(out=xn[:], in0=xt[:], scalar1=rstd0[:, gci:gci + 1], scalar2=None, op0=ALU.mult),
                       reads=[t_xt, t_r0[gci]], writes=[t_xn])
                  return xn, t_xn

              def fe_T(gci, grp, xn, t_xn):
                  pt, t_pt = pT.next()
                  for fc in range(8):
                      S.op("pe", lambda e, fc=fc: e.transpose(out=pt[:, fc, :], in_=xn[:, fc * 128:(fc + 1) * 128], identity=ident_bf[:]),
                           reads=[t_xn, t_const], writes=[t_pt])
                  hT, t_hT = hT_r.next()
                  for fc in range(8):
                      if fc % 4 != 3:
                          S.op("act", lambda e, fc=fc: e.activation(out=hT[:, fc, :], in_=pt[:, fc, :], func=AF.Identity,
                                                                    scale=A_pp[:, 0, grp, fc:fc + 1], bias=B_pp[:, 0, grp, fc:fc + 1]),
                               reads=[t_pt, t_AB], writes=[t_hT])
                  for fc in range(8):
                      if fc % 4 == 3:
                          S.op("dve", lambda e, fc=fc: e.tensor_scalar(out=hT[:, fc, :], in0=pt[:, fc, :], scalar1=A_pp[:, 0, grp, fc:fc + 1],
                                                                      scalar2=B_pp[:, 0, grp, fc:fc + 1], op0=ALU.mult, op1=ALU.add),
                               reads=[t_pt, t_AB], writes=[t_hT])
                  return hT, t_hT

              def proj_tok(hT, t_hT, col0):
                  bank, t_bank = pB.next()
                  for fc in range(8):
                      S.op("pe", lambda e, fc=fc: e.matmul(out=bank[:], lhsT=hT[:, fc, :], rhs=w_in[:, fc, col0:col0 + 512],
                                                           start=(fc == 0), stop=(fc == 7)), reads=[t_hT, t_wkv], writes=[t_bank])
                  return bank, t_bank

              def proj_feat(hT, t_hT, col0):
                  bank, t_bank = pB.next()
                  for j in range(4):
                      for fc in range(8):
                          S.op("pe", lambda e, fc=fc, j=j: e.matmul(out=bank[:, j * 128:(j + 1) * 128],
                                                                    lhsT=w_in[:, fc, col0 + j * 128:col0 + (j + 1) * 128], rhs=hT[:, fc, :],
                                                                    start=(fc == 0), stop=(fc == 7)), reads=[t_hT, t_win, t_wkv], writes=[t_bank])
                  return bank, t_bank

              def state_update(Sx, t_Sx, kx, t_kx, vb, t_vb, cdec):
                  bank, t_bank = pB.next()
                  for h in range(4):
                      hs = slice(h * 128, (h + 1) * 128)
                      S.op("pe", lambda e, hs=hs: e.matmul(out=bank[:, hs], lhsT=kx[:, hs], rhs=vb[:, hs], start=True, stop=True),
                           reads=[t_kx, t_vb], writes=[t_bank])
                  for h in range(4):
                      hs = slice(h * 128, (h + 1) * 128)
                      S.op("dve", lambda e, hs=hs, h=h: e.scalar_tensor_tensor(out=Sx[:, hs], in0=Sx[:, hs], scalar=cdec[h], in1=bank[:, hs],
                                                                               op0=ALU.mult, op1=ALU.add), reads=[t_Sx, t_bank], writes=[t_Sx])

              def kv_tok(hT, t_hT, kdec):
                  bank, t_bank = proj_tok(hT, t_hT, 512)
                  kx, t_kx = kf_r.next()
                  S.op("dve", lambda e: e.tensor_tensor(out=kx[:], in0=bank[:], in1=kdec, op=ALU.mult), reads=[t_bank, t_tab], writes=[t_kx])
                  bank2, t_bank2 = proj_tok(hT, t_hT, 1024)
                  vb, t_vb = vb_r.next()
                  S.op("act", lambda e: e.copy(out=vb[:], in_=bank2[:]), reads=[t_bank2], writes=[t_vb])
                  return kx, t_kx, vb, t_vb

              def l0_sequence(cb0, NCs, grp, nseq):
                  NC = NCs * nseq
                  Gt, t_G = G_bc.next()
                  S.dma("sp", Gt[:], g_hbm[0, grp:grp + 1, :].partition_broadcast(128), reads=[t_g], writes=[t_G])
                  if grp == 0:
                      S.dma("sp", S_b[:].rearrange("p (h e) -> p h e", h=4), st_b.rearrange("h d e -> d h e"), writes=[t_Sb])
                      S.dma("sp", S_f[:].rearrange("p (h e) -> p h e", h=4), st_f.rearrange("h d e -> d h e"), writes=[t_Sf])
                  def r_proj(ci):
                      hT, t_hT = fr[ci]
                      return kv_tok(hT, t_hT, kdecB)

                  fr = {}
                  ld = {}
                  for k_ in range(NC - 1, max(NC - 5, -1), -1):
                      ld[k_] = fe_load(cb0 + k_, True)
                  for k_ in range(NC - 1, max(NC - 4, -1), -1):
                      fr[k_] = fe_main(cb0 + k_, grp, *ld.pop(k_))
                  cur = r_proj(NC - 1)
                  for ci in reversed(range(NC)):
                      gci = cb0 + ci
                      nxt_kv = None
                      nrm = None
                      if ci >= 3:
                          nrm = fe_norm(gci - 3, *ld.pop(ci - 3))
                      if ci >= 1:
                          nxt_kv = r_proj(ci - 1)
                      kx, t_kx, vb, t_vb = cur
                      if grp == 1 and ci % NCs == NCs - 1:
                          S.op("pool", lambda e: e.memset(S_b[:], 0.0), writes=[t_Sb])
                      sw, t_sw = snapw_r.next()
                      S.op("act", lambda e, sw=sw: e.copy(out=sw[:], in_=S_b[:]), reads=[t_Sb], writes=[t_sw])
                      S.dma("pool", snap_hbm[gci], sw[:], reads=[t_sw], writes=[t_snap[gci]])
                      state_update(S_b, t_Sb, kx, t_kx, vb, t_vb, cB)
                      if ci >= 3:
                          fr[ci - 3] = fe_T(gci - 3, grp, *nrm)
                      if ci >= 4:
                          ld[ci - 4] = fe_load(gci - 4, True)
                      cur = nxt_kv
                      fr.pop(ci, None)
                      if grp == 1 and ci % NCs == 0:
                          S.dma("pool", new_sb[ci // NCs].rearrange("h d e -> d h e"), S_b[:].rearrange("p (h e) -> p h e", h=4),
                                reads=[t_Sb], is_output=True)
                  if grp == 0:
                      S.op("act", lambda e: e.copy(out=Sf_bf[:], in_=S_f[:]), reads=[t_Sf], writes=[t_Sfbf])
                  def stageA(ci):
                      hT, t_hT = fr[ci]
                      c = {}

                      def a0():
                          bank, t_bank = proj_feat(hT, t_hT, 2560)
                          cgs, t_cgs = cgs_r.next()
                          S.op("act", lambda e: e.copy(out=cgs[:], in_=bank[:]), reads=[t_bank], writes=[t_cgs])
                          bank2, t_bank2 = proj_feat(hT, t_hT, 3072)
                          z, t_z = z_r.next()
                          S.op("dve", lambda e: e.tensor_tensor(
                              out=z[:, :, 1:129], in0=bank2[:].rearrange("p (f t) -> p f t", f=4), in1=cgs[:].rearrange("p (f t) -> p f t", f=4), op=ALU.mult),
                              reads=[t_bank2, t_cgs], writes=[t_z])
                          if ci % NCs == 0:
                              S.op("pool", lambda e: e.memset(z[:, :, 0:1], 0.0), writes=[t_z])
                          c["z"] = (z, t_z)

                      def a1():
                          bank, t_bank = proj_feat(hT, t_hT, 0)
                          qT, t_qT = qT_r.next()
                          qTf, t_qTf = qTf_r.next()
                          qTb, t_qTb = qTb_r.next()
                          S.op("act", lambda e: e.copy(out=qT[:], in_=bank[:]), reads=[t_bank], writes=[t_qT])
                          S.op("dve", lambda e: e.tensor_tensor(out=qTf[:], in0=bank[:], in1=qdecF, op=ALU.mult), reads=[t_bank, t_tab], writes=[t_qTf])
                          S.op("dve", lambda e: e.tensor_tensor(out=qTb[:], in0=bank[:], in1=qdecB, op=ALU.mult), reads=[t_bank, t_tab], writes=[t_qTb])
                          bank2, t_bank2 = proj_feat(hT, t_hT, 512)
                          kT, t_kT = kT_r.next()
                          S.op("act", lambda e: e.copy(out=kT[:], in_=bank2[:]), reads=[t_bank2], writes=[t_kT])
                          c.update(qT=(qT, t_qT), qTf=(qTf, t_qTf), qTb=(qTb, t_qTb), kT=(kT, t_kT))

                      def a2():
                          bank, t_bank = proj_feat(hT, t_hT, 1536)
                          sga, t_sga = sga_r.next()
                          silu_psum(bank, t_bank, sga[:], t_sga)
                          bank2, t_bank2 = proj_feat(hT, t_hT, 3584)
                          sgb, t_sgb = sgb_r.next()
                          silu_psum(bank2, t_bank2, sgb[:], t_sgb)
                          c.update(sga=(sga, t_sga), sgb=(sgb, t_sgb))

                      def a3():
                          bank, t_bank = proj_feat(hT, t_hT, 2048)
                          u, t_u = u_r.next()
                          sgb, t_sgb = c["sgb"]
                          S.op("dve", lambda e: e.tensor_tensor(out=u[:], in0=bank[:], in1=sgb[:], op=ALU.mult), reads=[t_bank, t_sgb], writes=[t_u])
                          c["u"] = (u, t_u)
                          c["kv"] = kv_tok(hT, t_hT, kdecF)

                      return c, [a0, a1, a2, a3]

                  def stageB(ci, c):
                      gci = cb0 + ci
                      qT, t_qT = c["qT"]
                      qTf, t_qTf = c["qTf"]
                      qTb, t_qTb = c["qTb"]
                      kT, t_kT = c["kT"]
                      sga, t_sga = c["sga"]
                      kx, t_kx, vb, t_vb = c["kv"]
                      d = {}

                      def b0():
                          bank, t_bank = pB.next()
                          for h in range(4):
                              hs = slice(h * 128, (h + 1) * 128)
                              S.op("pe", lambda e, hs=hs: e.matmul(out=bank[:, hs], lhsT=kT[:, hs], rhs=qT[:, hs], start=True, stop=True),
                                   reads=[t_kT, t_qT], writes=[t_bank])
                          attm, t_attm = attm_r.next()
                          S.op("dve", lambda e: e.tensor_tensor(out=attm[:], in0=bank[:], in1=maskT, op=ALU.mult), reads=[t_bank, t_tab], writes=[t_attm])
                          snr, t_snr = snapr_r.next()
                          S.dma("sp", snr[:], snap_hbm[gci], reads=[t_snap[gci]], writes=[t_snr])
                          d.update(attm=(attm, t_attm), snr=(snr, t_snr))

                      def b1():
                          attm, t_attm = d["attm"]
                          snr, t_snr = d["snr"]
                          bank, t_bank = pB.next()
                          for h in range(4):
                              hs = slice(h * 128, (h + 1) * 128)
                              S.op("pe", lambda e, hs=hs: e.matmul(out=bank[:, hs], lhsT=vb[:, hs], rhs=attm[:, hs], start=True, stop=False),
                                   reads=[t_vb, t_attm], writes=[t_bank])
                              S.op("pe", lambda e, hs=hs: e.matmul(out=bank[:, hs], lhsT=Sf_bf[:, hs], rhs=qTf[:, hs], start=False, stop=False),
                                   reads=[t_Sfbf, t_qTf], writes=[t_bank])
                              S.op("pe", lambda e, hs=hs: e.matmul(out=bank[:, hs], lhsT=snr[:, hs], rhs=qTb[:, hs], start=False, stop=True),
                                   reads=[t_snr, t_qTb], writes=[t_bank])
                          osb, t_osb = osb_r.next()
                          osq, t_osq = osq_r.next()
                          S.op("act", lambda e: e.copy(out=osb[:], in_=bank[:]), reads=[t_bank], writes=[t_osb])
                          S.op("act", lambda e: e.activation(out=osq[:], in_=bank[:], func=AF.Square), reads=[t_bank], writes=[t_osq])
                          d.update(osb=(osb, t_osb), osq=(osq, t_osq))

                      def b2():
                          osb, t_osb = d["osb"]
                          osq, t_osq = d["osq"]
                          bank, t_bank = pB.next()
                          S.op("pe", lambda e: e.matmul(out=bank[:], lhsT=ones_bf[:], rhs=osq[:], start=True, stop=True), reads=[t_osq, t_const], writes=[t_bank])
                          ms, t_ms = ms_r.next()
                          S.op("act", lambda e: e.activation(out=ms[:], in_=bank[:], func=AF.Ln, scale=1.0 / 128, bias=eps_t[:]), reads=[t_bank, t_eps], writes=[t_ms])
                          S.op("act", lambda e: e.activation(out=ms[:], in_=ms[:], func=AF.Exp, scale=-0.5), reads=[t_ms], writes=[t_ms])
                          yat, t_yat = yat_r.next()
                          S.op("dve", lambda e: e.tensor_tensor(out=yat[:], in0=osb[:], in1=ms[:], op=ALU.mult), reads=[t_osb, t_ms], writes=[t_yat])
                          ymix, t_ymix = ymix_r.next()
                          S.op("pool", lambda e: e.tensor_tensor(
                              out=ymix[:, 0:4, :], in0=yat[:].rearrange("p (f t) -> p f t", f=4), in1=sga[:].rearrange("p (f t) -> p f t", f=4), op=ALU.mult),
                              reads=[t_yat, t_sga], writes=[t_ymix])
                          c["ymix"] = (ymix, t_ymix)

                      def b3():
                          state_update(S_f, t_Sf, kx, t_kx, vb, t_vb, cF)
                          S.op("act", lambda e: e.copy(out=Sf_bf[:], in_=S_f[:]), reads=[t_Sf], writes=[t_Sfbf])

                      return [b0, b1, b2, b3]

                  def stageC1(ci, c, cn):
                      z, t_z = c["z"]
                      u, t_u = c["u"] if "u" in c else (None, None)
                      if cn is not None:
                          zn, t_zn = cn["z"]
                          S.op("pool", lambda e: e.tensor_copy(out=z[:, :, 129:130], in_=zn[:, :, 1:2]), reads=[t_zn], writes=[t_z])
                          S.op("pool", lambda e: e.tensor_copy(out=zn[:, :, 0:1], in_=z[:, :, 128:129]), reads=[t_z], writes=[t_zn])
                      else:
                          S.op("pool", lambda e: e.memset(z[:, :, 129:130], 0.0), writes=[t_z])
                      zc, t_zc = zc_r.next()
                      for f in range(4):
                          S.op("dve", lambda e, f=f: e.tensor_scalar(out=zc[:, f, :], in0=z[:, f, 0:128], scalar1=cw[:, f, 0:1], scalar2=None, op0=ALU.mult),
                               reads=[t_z, t_tab], writes=[t_zc])
                          S.op("dve", lambda e, f=f: e.scalar_tensor_tensor(out=zc[:, f, :], in0=z[:, f, 1:129], scalar=cw[:, f, 1:2], in1=zc[:, f, :],
                                                                            op0=ALU.mult, op1=ALU.add), reads=[t_z, t_tab, t_zc], writes=[t_zc])
                          S.op("dve", lambda e, f=f: e.scalar_tensor_tensor(out=zc[:, f, :], in0=z[:, f, 2:130], scalar=cw[:, f, 2:3], in1=zc[:, f, :],
                                                                            op0=ALU.mult, op1=ALU.add), reads=[t_z, t_tab, t_zc], writes=[t_zc])
                      c["zc"] = (zc, t_zc)

                  def stageC2(ci, c):
                      gcp = cb0 + ci
                      ymix, t_ymix = c["ymix"]
                      u, t_u = c["u"]
                      zc, t_zc = c["zc"]
                      S.op("pool", lambda e: e.tensor_tensor(out=ymix[:, 4:8, :], in0=u[:].rearrange("p (f t) -> p f t", f=4), in1=zc[:], op=ALU.mult),
                           reads=[t_u, t_zc], writes=[t_ymix])
                      xres, t_xres = xres_r.next()
                      S.dma("sp", xres[:], x_all[gcp * 128:(gcp + 1) * 128, :], writes=[t_xres])
                      x1t, t_x1t = x1t_r.next()
                      for half in range(2):
                          bank, t_bank = pB.next()
                          cs = slice(half * 512, (half + 1) * 512)
                          for mc in range(8):
                              S.op("pe", lambda e, mc=mc, cs=cs, bank=bank: e.matmul(out=bank[:], lhsT=ymix[:, mc, :], rhs=w_out[:, mc, cs],
                                                                                    start=(mc == 0), stop=(mc == 7)),
                                   reads=[t_ymix, t_wout], writes=[t_bank])
                          S.op("dve", lambda e, cs=cs, bank=bank: e.tensor_tensor(out=x1t[:, cs], in0=bank[:], in1=Gt[:, cs], op=ALU.mult),
                               reads=[t_bank, t_G], writes=[t_x1t])
                      S.op("pool", lambda e: e.tensor_tensor(out=x1t[:], in0=x1t[:], in1=xres[:], op=ALU.add), reads=[t_x1t, t_xres], writes=[t_x1t])
                      S.op("act", lambda e: e.activation(out=junk[:], in_=x1t[:], func=AF.Square, accum_out=ssq1[:, gcp:gcp + 1]),
                           reads=[t_x1t], writes=[t_junk, t_r1[gcp]])
                      rstd_from_ssq(ssq1[:, gcp:gcp + 1], lnt[:, gcp:gcp + 1], rstd1[:, gcp:gcp + 1], float(D), t_r1[gcp])
                      S.dma("pool", x1_hbm[gcp * 128:(gcp + 1) * 128, :], x1t[:], reads=[t_x1t], writes=[t_x1[gcp]], is_output=dbg)

                  fr = {}
                  for k_ in range(min(3, NC)):
                      fr[k_] = frontend(cb0 + k_, grp, False)
                  ctx = {}
                  ctx[0], pa = stageA(0)
                  for p_ in pa:
                      p_()
                  for ci in range(NC):
                      pa = []
                      nrm = None
                      if ci + 3 < NC:
                          nrm = fe_norm(cb0 + ci + 3, *fe_load(cb0 + ci + 3, False))
                      if ci + 1 < NC:
                          ctx[ci + 1], pa = stageA(ci + 1)
                      pb = stageB(ci, ctx[ci])
                      if grp == 1 and ci % NCs == 0:
                          S.op("pool", lambda e: e.memset(S_f[:], 0.0), writes=[t_Sf])
                          S.op("act", lambda e: e.copy(out=Sf_bf[:], in_=S_f[:]), reads=[t_Sf], writes=[t_Sfbf])
                      pb[0]()
                      if pa:
                          pa[0]()
                      pb[1]()
                      if pa:
                          pa[1]()
                      stageC1(ci, ctx[ci], ctx.get(ci + 1) if ci % NCs != NCs - 1 else None)
                      pb[2]()
                      if pa:
                          pa[2]()
                      pb[3]()
                      if grp == 1 and ci % NCs == NCs - 1:
                          S.dma("pool", new_sf[ci // NCs].rearrange("h d e -> d h e"), S_f[:].rearrange("p (h e) -> p h e", h=4),
                                reads=[t_Sf], is_output=True)
                      if pa:
                          pa[3]()
                      if ci + 3 < NC:
                          fr[ci + 3] = fe_T(cb0 + ci + 3, grp, *nrm)
                      stageC2(ci, ctx[ci])
                      ctx.pop(ci - 1, None)
                      fr.pop(ci, None)

              if run_sample:
                  l0_sequence(0, T_S // 128, 0, 1)
              if n_prompts > 0:
                  l0_sequence(T_S // 128, T_P // 128, 1, n_prompts)
              S.barrier()

        sw.close()

        if do_l1:
          with contextlib.ExitStack() as s1:
            SCALE = 192.0 ** -0.5
            w1 = sb(s1, "w1", [128, 8, 1856], BF16)
            qup = sb(s1, "qup", [128, 3, 2112], BF16)
            kvup = sb(s1, "kvup", [128, 2, 2048], BF16)
            wo1 = sb(s1, "wo1", [128, 8, 1024], BF16)
            t_w1, t_qup, t_kvup, t_wo1 = T(), T(), T(), T()
            S.op("pool", lambda e: e.memset(w1[:, :, 1792:1856], 0.0), writes=[t_w1])
            S.op("pool", lambda e: e.memset(qup[:, :, 2048:2112], 0.0), writes=[t_qup])
            for c0 in range(0, 1792, 448):
                S.dma("pool", w1[:, :, c0:c0 + 448], w_in1e[:, c0:c0 + 448].rearrange("(c p) n -> p c n", p=128), writes=[t_w1])
            for c0 in range(0, 2048, 512):
                S.dma("pool", qup[:, :, c0:c0 + 512], q_up_e[:, c0:c0 + 512].rearrange("(c p) n -> p c n", p=128), writes=[t_qup])
                S.dma("pool", kvup[:, :, c0:c0 + 512], kv_up_d[:, c0:c0 + 512].rearrange("(c p) n -> p c n", p=128), writes=[t_kvup])
            for c0 in range(0, 1024, 512):
                S.dma("pool", wo1[:, :, c0:c0 + 512], w_out1_d[:, c0:c0 + 512].rearrange("(c p) n -> p c n", p=128), writes=[t_wo1])
            qng = sb(s1, "qng", [128, 3], F32)
            kvng = sb(s1, "kvng", [128, 2], F32)
            rtab = sb(s1, "rtab", [64, 2, 64], F32)
            kvng_bc = sb(s1, "kvng_bc", [128, 256], F32)
            fng_bc = sb(s1, "fng_bc", [128, 1024], F32)
            t_c1 = T()
            S.dma("sp", qng[:], qng_pp, writes=[t_c1])
            S.dma("sp", kvng[:], kvng_pp, writes=[t_c1])
            S.dma("sp", rtab[:], ropetab, writes=[t_c1])
            S.dma("sp", kvng_bc[:], kvng_row[0].partition_broadcast(128), writes=[t_c1])
            S.dma("sp", fng_bc[:], fng_row[0].partition_broadcast(128), writes=[t_c1])
            G1_bc = Ring([sb(s1, "G1_bc%d" % i, [128, 1024], F32) for i in range(2)])
            LKMAX = PAST + T_S
            ckvT = sb(s1, "ckvT", [128, 2, LKMAX], BF16)
            krT = sb(s1, "krT", [128, LKMAX], BF16)
            qlnT = sb(s1, "qlnT", [128, 3, T_S], BF16)
            t_ckvT, t_krT, t_qlnT = T(), T(), T()
            S.op("pool", lambda e: e.memset(krT[64:128, :], 0.0), writes=[t_krT])
            big_r = Ring([sb(s1, "big%d" % i, [128, 8, 512], BF16) for i in range(2)])
            xt1_r = Ring([sb(s1, "xt1_%d" % i, [128, 1024], F32) for i in range(3)])
            junk1 = sb(s1, "junk1", [128, 1024], BF16)
            t_junk1 = T()
            ssq2 = sb(s1, "ssq2", [128, NCH], F32)
            ln2 = sb(s1, "ln2", [128, NCH], F32)
            rstd2 = sb(s1, "rstd2", [128, NCH], F32)
            t_r2 = [T() for _ in range(NCH)]
            t_gsc = [T() for _ in range(NCH)]
            t_osc = {}

            def rope_rotate(src, t_src, srcsw, t_srcsw, dst, t_dst, tok0, ntok, tmp1, t_tmp1, tmp2, t_tmp2):
                r0, nr = tok0 // 64, ntok // 64
                for half in range(2):
                    ps_ = slice(32 * half, 32 * half + 32)
                    if half == 0:
                        cb = rtab[ps_, 0, r0:r0 + nr].unsqueeze(2).broadcast_to([32, nr, 64])
                        sn = rtab[ps_, 1, r0:r0 + nr].unsqueeze(2).broadcast_to([32, nr, 64])
                    else:
                        cb = rtab[ps_, 0, :].unsqueeze(1).broadcast_to([32, nr, 64])
                        sn = rtab[ps_, 1, :].unsqueeze(1).broadcast_to([32, nr, 64])
                    v3 = lambda ap: ap.rearrange("p (r c) -> p r c", c=64)
                    S.op("dve", lambda e, ps_=ps_, cb=cb: e.tensor_tensor(out=v3(tmp1[ps_, 0:ntok]), in0=v3(src[ps_, :]), in1=cb, op=ALU.mult),
                         reads=[t_src, t_c1], writes=[t_tmp1])
                    S.op("dve", lambda e, ps_=ps_, sn=sn: e.tensor_tensor(out=v3(tmp2[ps_, 0:ntok]), in0=v3(srcsw[ps_, :]), in1=sn, op=ALU.mult),
                         reads=[t_srcsw, t_c1], writes=[t_tmp2])
                S.op("pool", lambda e: e.tensor_tensor(out=dst, in0=tmp1[0:64, 0:ntok], in1=tmp2[0:64, 0:ntok], op=ALU.add),
                     reads=[t_tmp1, t_tmp2], writes=[t_dst])

            def silu_to(bank, t_bank, et, t_et, out_ap, t_out):
                S.op("act", lambda e: e.activation(out=et[:], in_=bank[:], func=AF.Exp, scale=-1.0), reads=[t_bank], writes=[t_et])
                S.op("act", lambda e: e.activation(out=et[:], in_=et[:], func=AF.Ln, bias=one_t[:]), reads=[t_et, t_eps], writes=[t_et])
                S.op("act", lambda e: e.activation(out=et[:], in_=et[:], func=AF.Exp, scale=-1.0), reads=[t_et], writes=[t_et])
                S.op("dve", lambda e: e.tensor_tensor(out=out_ap, in0=bank[:].rearrange("p (f t) -> p f t", f=4), in1=et[:].rearrange("p (f t) -> p f t", f=4), op=ALU.mult),
                     reads=[t_bank, t_et], writes=[t_out])

            def l1_sequence(cb0, NC, grp, nseq):
                T_ = NC * 128
                koff = PAST if grp == 0 else 0
                Lk = koff + T_
                NKB = Lk // 128
                QW = 512 if grp == 0 else 256
                NQG = T_ // QW
                row0 = cb0 * 128
                Gt, t_G = G1_bc.next()
                S.dma("sp", Gt[:], g_hbm[1, grp, :].partition_broadcast(128), reads=[t_g], writes=[t_G])
                with contextlib.ExitStack() as sp1:
                    xn_r = Ring([sb(sp1, "xn1_%d" % i, [128, 1024], BF16) for i in range(3)])
                    hT_r = Ring([sb(sp1, "hT1_%d" % i, [128, 8, 128], BF16) for i in range(3)])
                    sq_r = Ring([sb(sp1, "sq1_%d" % i, [128, 384], BF16) for i in range(4)])
                    rs_r = Ring([sb(sp1, "rs1_%d" % i, [128, 128], F32) for i in range(4)])
                    et_r = Ring([sb(sp1, "et1_%d" % i, [128, 512], F32) for i in range(2)])
                    rt1 = sb(sp1, "rt1", [64, 128], F32)
                    rt2 = sb(sp1, "rt2", [64, 128], F32)
                    t_rt1, t_rt2 = T(), T()
                    co_r = Ring([sb(sp1, "co%d" % i, [128, 320], F32) for i in range(2)])
                    sk = sb(sp1, "sk", [128, 4], F32)
                    t_sk = T()
                    cx_r = Ring([sb(sp1, "cx%d" % i, [128, 320], F32) for i in range(2)])
                    cxb_r = Ring([sb(sp1, "cxb%d" % i, [128, 320], BF16) for i in range(2)])
                    if grp == 0:
                        for blk in range(PAST // 128):
                            cx, t_cx = cx_r.next()
                            S.dma("sp", cx[:, 64:320], c_ckv[blk * 128:(blk + 1) * 128, :], writes=[t_cx])
                            S.dma("sp", cx[:, 0:64], c_kr[blk * 128:(blk + 1) * 128, :], writes=[t_cx])
                            cxb, t_cxb = cxb_r.next()
                            S.op("dve", lambda e, cx=cx, cxb=cxb: e.tensor_copy(out=cxb[:], in_=cx[:]), reads=[t_cx], writes=[t_cxb])
                            pt, t_pt = pT.next()
                            for j in range(2):
                                S.op("pe", lambda e, j=j, pt=pt, cxb=cxb: e.transpose(out=pt[:, j, :], in_=cxb[:, 64 + j * 128:64 + (j + 1) * 128], identity=ident_bf[:]),
                                     reads=[t_cxb, t_const], writes=[t_pt])
                            S.op("pe", lambda e, pt=pt, cxb=cxb: e.transpose(out=pt[:, 2, :], in_=cxb[:, 0:128], identity=ident_bf[:]),
                                 reads=[t_cxb, t_const], writes=[t_pt])
                            S.op("act", lambda e, pt=pt, blk=blk: e.copy(out=ckvT[:, :, blk * 128:(blk + 1) * 128], in_=pt[:, 0:2, :]),
                                 reads=[t_pt], writes=[t_ckvT])
                            S.op("act", lambda e, pt=pt, blk=blk: e.copy(out=krT[0:64, blk * 128:(blk + 1) * 128], in_=pt[0:64, 2, :]),
                                 reads=[t_pt], writes=[t_krT])

                    def frontend1(gci):
                        xt, t_xt = xt1_r.next()
                        S.dma("sp", xt[:], x1_hbm[gci * 128:(gci + 1) * 128, :], reads=[t_x1[gci]], writes=[t_xt])
                        xn, t_xn = xn_r.next()
                        S.op("dve", lambda e: e.tensor_scalar(out=xn[:], in0=xt[:], scalar1=rstd1[:, gci:gci + 1], scalar2=None, op0=ALU.mult),
                             reads=[t_xt, t_r1[gci]], writes=[t_xn])
                        pt, t_pt = pT.next()
                        for fc in range(8):
                            S.op("pe", lambda e, fc=fc: e.transpose(out=pt[:, fc, :], in_=xn[:, fc * 128:(fc + 1) * 128], identity=ident_bf[:]),
                                 reads=[t_xn, t_const], writes=[t_pt])
                        hT, t_hT = hT_r.next()
                        for fc in range(8):
                            if fc % 2 == 0:
                                S.op("act", lambda e, fc=fc: e.activation(out=hT[:, fc, :], in_=pt[:, fc, :], func=AF.Identity,
                                                                          scale=A_pp[:, 1, grp, fc:fc + 1], bias=B_pp[:, 1, grp, fc:fc + 1]),
                                     reads=[t_pt, t_AB], writes=[t_hT])
                        for fc in range(8):
                            if fc % 2 == 1:
                                S.op("dve", lambda e, fc=fc: e.tensor_scalar(out=hT[:, fc, :], in0=pt[:, fc, :], scalar1=A_pp[:, 1, grp, fc:fc + 1],
                                                                            scalar2=B_pp[:, 1, grp, fc:fc + 1], op0=ALU.mult, op1=ALU.add),
                                     reads=[t_pt, t_AB], writes=[t_hT])
                        return hT, t_hT

                    def pf(hT, t_hT, cols):
                        bank, t_bank = pB.next()
                        for j, (c0, M) in enumerate(cols):
                            for fc in range(8):
                                S.op("pe", lambda e, fc=fc, j=j, c0=c0, M=M: e.matmul(out=bank[:, j * 128:(j + 1) * 128], lhsT=w1[:, fc, c0:c0 + 128], rhs=hT[:, fc, :],
                                                                                    start=(fc == 0), stop=(fc == 7)), reads=[t_hT, t_w1], writes=[t_bank])
                        return bank, t_bank

                    def feat_rms_a(bank, t_bank, nch):
                        sq, t_sq = sq_r.next()
                        S.op("act", lambda e: e.activation(out=sq[:, 0:nch * 128], in_=bank[:, 0:nch * 128], func=AF.Square), reads=[t_bank], writes=[t_sq])
                        return sq, t_sq

                    def feat_rms_b1(sq, t_sq, nch):
                        b2, t_b2 = pB.next()
                        for j in range(nch):
                            S.op("pe", lambda e, j=j: e.matmul(out=b2[:, 0:128], lhsT=ones_bf[:], rhs=sq[:, j * 128:(j + 1) * 128], start=(j == 0), stop=(j == nch - 1)),
                                 reads=[t_sq, t_const], writes=[t_b2])
                        return b2, t_b2

                    def feat_rms_b2(bank, t_bank, b2, t_b2, nch, n, gtab, dst_fn, t_dst):
                        rs, t_rs = rs_r.next()
                        S.op("act", lambda e: e.activation(out=rs[:, 0:128], in_=b2[:, 0:128], func=AF.Ln, scale=1.0 / n, bias=eps_t[:]), reads=[t_b2, t_eps], writes=[t_rs])
                        S.op("act", lambda e: e.activation(out=rs[:, 0:128], in_=rs[:, 0:128], func=AF.Exp, scale=-0.5), reads=[t_rs], writes=[t_rs])
                        for j in range(nch):
                            S.op("dve", lambda e, j=j: e.scalar_tensor_tensor(out=dst_fn(j), in0=bank[:, j * 128:(j + 1) * 128], scalar=gtab[:, j:j + 1], in1=rs[:, 0:128],
                                                                              op0=ALU.mult, op1=ALU.mult), reads=[t_bank, t_rs, t_c1], writes=[t_dst])

                    fq = [frontend1(cb0)]
                    if nseq * NC > 1:
                        fq.append(frontend1(cb0 + 1))
                    gsl = [None, None]

                    def stream_chunk(si, ci, hT, t_hT, mid_hook):
                        pi = si
                        lc = si * NC + ci
                        gci = cb0 + lc
                        tk = si * Lk + koff + ci * 128
                        gs, t_gs = gsl
                        bankq, t_bankq = pf(hT, t_hT, [(0, 128), (128, 128), (256, 128)])
                        sqq, t_sqq = feat_rms_a(bankq, t_bankq, 3)
                        bankk, t_bankk = pf(hT, t_hT, [(384, 128), (512, 128), (640, 128), (1728, 128)])
                        sqk, t_sqk = feat_rms_a(bankk, t_bankk, 2)
                        mid_hook()
                        b2q, t_b2q = feat_rms_b1(sqq, t_sqq, 3)
                        if lc % 4 == 0:
                            gs, t_gs = big_r.next()
                            gsl[0], gsl[1] = gs, t_gs
                        off = (lc % 4) * 128
                        bankg0, t_bankg0 = pf(hT, t_hT, [(704 + k * 128, 128) for k in range(4)])
                        feat_rms_b2(bankq, t_bankq, b2q, t_b2q, 3, 384.0, qng, lambda j: qlnT[:, j, lc * 128:(lc + 1) * 128], t_qlnT)
                        b2k, t_b2k = feat_rms_b1(sqk, t_sqk, 2)
                        et, t_et = et_r.next()
                        silu_to(bankg0, t_bankg0, et, t_et, gs[:, 0:4, off:off + 128], t_gs)
                        bankg1, t_bankg1 = pf(hT, t_hT, [(704 + 512 + k * 128, 128) for k in range(4)])
                        feat_rms_b2(bankk, t_bankk, b2k, t_b2k, 2, 256.0, kvng, lambda j: ckvT[:, j, tk:tk + 128], t_ckvT)
                        if grp == 0:
                            rope_rotate(bankk[0:64, 256:384], t_bankk, bankk[0:64, 384:512], t_bankk, krT[0:64, tk:tk + 128], t_krT,
                                        ci * 128, 128, rt1, t_rt1, rt2, t_rt2)
                        else:
                            S.op("act", lambda e: e.copy(out=krT[0:64, tk:tk + 128], in_=bankk[0:64, 256:384]), reads=[t_bankk], writes=[t_krT])
                        et, t_et = et_r.next()
                        silu_to(bankg1, t_bankg1, et, t_et, gs[:, 4:8, off:off + 128], t_gs)
                        if lc % 4 == 3 or lc == nseq * NC - 1:
                            w = (lc % 4 + 1) * 128
                            r0 = row0 + (lc // 4) * 512
                            S.dma("pool", gsc[:, :, r0:r0 + w].rearrange("h p t -> p h t"), gs[:, :, 0:w], reads=[t_gs], writes=[t_gsc[(r0 // 512)]])
                        if grp == 1:
                            bank, t_bank = pB.next()
                            for fc in range(8):
                                S.op("pe", lambda e, fc=fc, bank=bank: e.matmul(out=bank[:, 0:320], lhsT=hT[:, fc, :], rhs=w1[:, fc, 384:704], start=(fc == 0), stop=(fc == 7)),
                                     reads=[t_hT, t_w1], writes=[t_bank])
                            co, t_co = co_r.next()
                            S.op("act", lambda e, bank=bank: e.activation(out=junk1[:, 0:256], in_=bank[:, 0:256], func=AF.Square, accum_out=sk[:, 0:1]),
                                 reads=[t_bank], writes=[t_junk1, t_sk])
                            S.op("act", lambda e: e.activation(out=sk[:, 1:2], in_=sk[:, 0:1], func=AF.Ln, scale=1.0 / 256, bias=eps_t[:]), reads=[t_sk, t_eps], writes=[t_sk])
                            S.op("act", lambda e: e.activation(out=sk[:, 2:3], in_=sk[:, 1:2], func=AF.Exp, scale=-0.5), reads=[t_sk], writes=[t_sk])
                            S.op("dve", lambda e, bank=bank, co=co: e.scalar_tensor_tensor(out=co[:, 0:256], in0=bank[:, 0:256], scalar=sk[:, 2:3], in1=kvng_bc[:],
                                                                                           op0=ALU.mult, op1=ALU.mult), reads=[t_bank, t_sk, t_c1], writes=[t_co])
                            S.op("act", lambda e, bank=bank, co=co: e.copy(out=co[:, 256:320], in_=bank[:, 256:320]), reads=[t_bank], writes=[t_co])
                            S.dma("pool", new_ckv[pi, ci * 128:(ci + 1) * 128, :], co[:, 0:256], reads=[t_co], is_output=True)
                            S.dma("pool", new_kr[pi, ci * 128:(ci + 1) * 128, :], co[:, 256:320], reads=[t_co], is_output=True)

                    for lc_ in range(nseq * NC):
                        hT, t_hT = fq.pop(0)

                        def hook(lc_=lc_):
                            if lc_ + 2 < nseq * NC:
                                fq.append(frontend1(cb0 + lc_ + 2))
                        stream_chunk(lc_ // NC, lc_ % NC, hT, t_hT, hook)
                    S.barrier()
                with contextlib.ExitStack() as sp2:
                    NKV = 1 if grp == 0 else 2
                    kT_ring = Ring([sb(sp2, "kTh%d" % i, [128, Lk], BF16) for i in range(NKV)])
                    V_ring = Ring([sb(sp2, "Vh%d" % i, [128, Lk // 128, 128], BF16) for i in range(NKV)])
                    qT_r = Ring([sb(sp2, "qTg%d" % i, [128, 512], BF16) for i in range(2)])
                    qr_r = Ring([sb(sp2, "qrg%d" % i, [128, 512], BF16) for i in range(2)])
                    for _b, _t in zip(qr_r.bufs, qr_r.ts):
                        S.op("pool", lambda e, _b=_b: e.memset(_b[64:128, :], 0.0), writes=[_t])
                    PT_r = Ring([sb(sp2, "PT%d" % i, [128, 512], BF16) for i in range(4)])
                    rs_r = Ring([sb(sp2, "rsa%d" % i, [128, 512], F32) for i in range(1)])
                    tt_r = Ring([sb(sp2, "tta%d" % i, [128, 512], F32) for i in range(1)])
                    gt_r = Ring([sb(sp2, "gt%d" % i, [128, 512], BF16) for i in range(2)])
                    ot_r = Ring([sb(sp2, "ot%d" % i, [128, 512], BF16) for i in range(2)])
                    accD_r = Ring([sb(sp2, "accD%d" % i, [128, 512], F32) for i in range(2)])
                    SUM_MOD = 4
                    POOL_SUMS = False
                    accP_r = Ring([sb(sp2, "accP%d" % i, [128, 512], F32) for i in range(1)])
                    ra1 = sb(sp2, "ra1", [64, 512], F32)
                    ra2 = sb(sp2, "ra2", [64, 512], F32)
                    t_ra1, t_ra2 = T(), T()
                    ST_AHEAD = 2
                    stR = pB.sub([0, 1, 2])
                    accR = pB.sub([3, 4, 5])
                    expR = Ring([pT.bufs[i][:].rearrange("p a b -> p (a b)").bitcast(F32) for i in range(2)])
                    expR.ts = [pT.ts[i] for i in range(2)]
                    def expand_kv(si, h):
                        kTh, t_kTh = kT_ring.next()
                        Vh, t_Vh = V_ring.next()
                        kb_ = si * Lk
                        for k0 in range(0, Lk, 512):
                            kw = min(512, Lk - k0)
                            bank, t_bank = expR.next()
                            for j in range(2):
                                S.op("pe", lambda e, j=j, bank=bank, k0=k0, kw=kw: e.matmul(out=bank[:, 0:kw], lhsT=kvup[:, j, h * 256:h * 256 + 128], rhs=ckvT[:, j, kb_ + k0:kb_ + k0 + kw],
                                                                                         start=(j == 0), stop=(j == 1)), reads=[t_kvup, t_ckvT], writes=[t_bank])
                            S.op("act", lambda e, bank=bank, k0=k0, kw=kw: e.copy(out=kTh[:, k0:k0 + kw], in_=bank[:, 0:kw]), reads=[t_bank], writes=[t_kTh])
                        for kb0 in range(0, NKB, 4):
                            nb = min(4, NKB - kb0)
                            bank, t_bank = expR.next()
                            for b in range(nb):
                                kb = kb0 + b
                                for j in range(2):
                                    S.op("pe", lambda e, j=j, bank=bank, kb=kb, b=b: e.matmul(out=bank[:, b * 128:(b + 1) * 128], lhsT=ckvT[:, j, kb_ + kb * 128:kb_ + (kb + 1) * 128],
                                                                                           rhs=kvup[:, j, h * 256 + 128:h * 256 + 256], start=(j == 0), stop=(j == 1)),
                                         reads=[t_kvup, t_ckvT], writes=[t_bank])
                            S.op("dve", lambda e, bank=bank, kb0=kb0, nb=nb: e.tensor_copy(out=Vh[:, kb0:kb0 + nb, :], in_=bank[:, 0:nb * 128].rearrange("p (b e) -> p b e", e=128)),
                                 reads=[t_bank], writes=[t_Vh])
                        return (kTh, t_kTh, Vh, t_Vh)

                    def expand_q(si, h, qg):
                        q0 = si * T_ + qg * QW
                        rows = slice(row0 + q0, row0 + q0 + QW)
                        gt, t_gt = gt_r.next()
                        S.dma("sp", gt[:, 0:QW], gsc[h, :, rows], reads=[t_gsc[(row0 + q0) // 512]], writes=[t_gt])
                        bank, t_bank = expR.next()
                        for j in range(3):
                            S.op("pe", lambda e, j=j: e.matmul(out=bank[:, 0:QW], lhsT=qup[:, j, h * 256:h * 256 + 128], rhs=qlnT[:, j, q0:q0 + QW],
                                                              start=(j == 0), stop=(j == 2)), reads=[t_qup, t_qlnT], writes=[t_bank])
                        qTg, t_qTg = qT_r.next()
                        S.op("act", lambda e: e.copy(out=qTg[:, 0:QW], in_=bank[:, 0:QW]), reads=[t_bank], writes=[t_qTg])
                        bank_r, t_bank_r = expR.next()
                        for j in range(3):
                            S.op("pe", lambda e, j=j: e.matmul(out=bank_r[:, 0:QW], lhsT=qup[:, j, h * 256 + 128:h * 256 + 256], rhs=qlnT[:, j, q0:q0 + QW],
                                                              start=(j == 0), stop=(j == 2)), reads=[t_qup, t_qlnT], writes=[t_bank_r])
                        qrg, t_qrg = qr_r.next()
                        if grp == 0:
                            bank_s, t_bank_s = expR.next()
                            for j in range(3):
                                S.op("pe", lambda e, j=j: e.matmul(out=bank_s[:, 0:QW], lhsT=qup[:, j, h * 256 + 192:h * 256 + 320], rhs=qlnT[:, j, q0:q0 + QW],
                                                                  start=(j == 0), stop=(j == 2)), reads=[t_qup, t_qlnT], writes=[t_bank_s])
                            rope_rotate(bank_r[0:64, 0:QW], t_bank_r, bank_s[0:64, 0:QW], t_bank_s, qrg[0:64, 0:QW], t_qrg, qg * QW, QW, ra1, t_ra1, ra2, t_ra2)
                        else:
                            S.op("act", lambda e: e.copy(out=qrg[0:64, 0:QW], in_=bank_r[0:64, 0:QW]), reads=[t_bank_r], writes=[t_qrg])
                        return (qTg, t_qTg, qrg, t_qrg, gt, t_gt)

                    def do_qg(si, h, qg, kvb, pre, nxt_item):
                        q0 = si * T_ + qg * QW
                        rows = slice(row0 + q0, row0 + q0 + QW)
                        qTg, t_qTg, qrg, t_qrg, gt, t_gt = pre
                        kTh, t_kTh, Vh, t_Vh = kvb
                        kb_ = si * Lk
                        Ob, t_Ob = accR.next()
                        Sb, t_Sb = accR.next()
                        dve_blocks = [kb for kb in range(NKB) if kb % SUM_MOD != SUM_MOD - 1]
                        pe_blocks = [kb for kb in range(NKB) if kb % SUM_MOD == SUM_MOD - 1]
                        accD, t_accD = accD_r.next()
                        accP, t_accP = accP_r.next()
                        res = [None]

                        def emit_st(kb):
                            ks = slice(kb * 128, (kb + 1) * 128)
                            stb, t_stb = stR.next()
                            S.op("pe", lambda e: e.matmul(out=stb[:, 0:QW], lhsT=kTh[:, ks], rhs=qTg[:, 0:QW], start=True, stop=False),
                                 reads=[t_kTh, t_qTg], writes=[t_stb])
                            ks2 = slice(kb_ + kb * 128, kb_ + (kb + 1) * 128)
                            S.op("pe", lambda e: e.matmul(out=stb[:, 0:QW], lhsT=krT[:, ks2], rhs=qrg[:, 0:QW], start=False, stop=True),
                                 reads=[t_krT, t_qrg], writes=[t_stb])
                            PT, t_PT = PT_r.next()
                            S.op("act", lambda e: e.activation(out=PT[:, 0:QW], in_=stb[:, 0:QW], func=AF.Exp, scale=SCALE), reads=[t_stb], writes=[t_PT])
                            if POOL_SUMS and kb in pe_blocks:
                                if kb == pe_blocks[0]:
                                    S.op("pool", lambda e: e.tensor_copy(out=accP[:, 0:QW], in_=PT[:, 0:QW]), reads=[t_PT], writes=[t_accP])
                                else:
                                    S.op("pool", lambda e: e.tensor_tensor(out=accP[:, 0:QW], in0=accP[:, 0:QW], in1=PT[:, 0:QW], op=ALU.add),
                                         reads=[t_PT, t_accP], writes=[t_accP])
                            if kb in dve_blocks:
                                if kb == dve_blocks[0]:
                                    S.op("dve", lambda e: e.tensor_copy(out=accD[:, 0:QW], in_=PT[:, 0:QW]), reads=[t_PT], writes=[t_accD])
                                else:
                                    S.op("dve", lambda e: e.tensor_tensor(out=accD[:, 0:QW], in0=accD[:, 0:QW], in1=PT[:, 0:QW], op=ALU.add),
                                         reads=[t_PT, t_accD], writes=[t_accD])
                            return PT, t_PT

                        def emit_pv(kb, PT, t_PT):
                            S.op("pe", lambda e: e.matmul(out=Ob[:, 0:QW], lhsT=Vh[:, kb, :], rhs=PT[:, 0:QW], start=(kb == 0), stop=(kb == NKB - 1)),
                                 reads=[t_Vh, t_PT], writes=[t_Ob])
                            if kb in pe_blocks and not POOL_SUMS:
                                S.op("pe", lambda e: e.matmul(out=Sb[:, 0:QW], lhsT=ones_bf[:], rhs=PT[:, 0:QW], start=(kb == pe_blocks[0]), stop=False),
                                     reads=[t_const, t_PT], writes=[t_Sb])

                        pend = [emit_st(kb) for kb in range(min(ST_AHEAD, NKB))]
                        for kb in range(NKB):
                            if kb + ST_AHEAD < NKB:
                                pend.append(emit_st(kb + ST_AHEAD))
                            emit_pv(kb, *pend.pop(0))
                            if kb == (NKB * 2) // 3 and nxt_item is not None:
                                if (nxt_item[0], nxt_item[1]) == (si, h):
                                    res[0] = (kvb, expand_q(*nxt_item))
                                elif NKV == 2:
                                    res[0] = (expand_kv(nxt_item[0], nxt_item[1]), expand_q(*nxt_item))
                        if POOL_SUMS and pe_blocks:
                            S.op("pe", lambda e: e.matmul(out=Sb[:, 0:QW], lhsT=ones_f32[:], rhs=accP[:, 0:QW], start=True, stop=False),
                                 reads=[t_const, t_accP], writes=[t_Sb])
                        S.op("pe", lambda e: e.matmul(out=Sb[:, 0:QW], lhsT=ones_f32[:], rhs=accD[:, 0:QW], start=(len(pe_blocks) == 0), stop=True),
                             reads=[t_const, t_accD], writes=[t_Sb])
                        rs, t_rs = rs_r.next()
                        S.op("act", lambda e: e.activation(out=rs[:, 0:QW], in_=Sb[:, 0:QW], func=AF.Ln), reads=[t_Sb], writes=[t_rs])
                        S.op("act", lambda e: e.activation(out=rs[:, 0:QW], in_=rs[:, 0:QW], func=AF.Exp, scale=-1.0), reads=[t_rs], writes=[t_rs])
                        tt, t_tt = tt_r.next()
                        S.op("dve", lambda e: e.tensor_tensor(out=tt[:, 0:QW], in0=Ob[:, 0:QW], in1=rs[:, 0:QW], op=ALU.mult), reads=[t_Ob, t_rs], writes=[t_tt])
                        ot, t_ot = ot_r.next()
                        S.op("pool", lambda e: e.tensor_tensor(out=ot[:, 0:QW], in0=tt[:, 0:QW], in1=gt[:, 0:QW], op=ALU.mult), reads=[t_tt, t_gt], writes=[t_ot])
                        key = (h, (row0 + q0) // 256)
                        t_osc[key] = T()
                        S.dma("pool", osc[h, :, rows], ot[:, 0:QW], reads=[t_ot], writes=[t_osc[key]])
                        return res[0]

                    items = [(si, h, qg) for si in range(nseq) for h in range(8) for qg in range(NQG)]
                    pre = None
                    for i, (si, h, qg) in enumerate(items):
                        if pre is None:
                            pre = (expand_kv(si, h), expand_q(si, h, qg))
                        nxt_item = items[i + 1] if i + 1 < len(items) else None
                        pre = do_qg(si, h, qg, pre[0], pre[1], nxt_item)
                    S.barrier()
                with contextlib.ExitStack() as sp3:
                    x2_r = Ring([sb(sp3, "x2_%d" % i, [128, 1024], F32) for i in range(4)])
                    yo_r = Ring([sb(sp3, "yo_%d" % i, [128, 1024], F32) for i in range(3)])
                    xr_r = Ring([sb(sp3, "xr_%d" % i, [128, 1024], F32) for i in range(4)])
                    GWc = QW // 128
                    def out_group(g0):
                        r0 = row0 + g0 * 128
                        ogt, t_ogt = big_r.next()
                        deps = [t_osc[(h, r0 // 256)] for h in range(8)]
                        S.dma("sp", ogt[:, :, 0:QW], osc[:, :, r0:r0 + QW].rearrange("h p t -> p h t"), reads=deps, writes=[t_ogt])
                        for c in range(GWc):
                            st_ = out_chunk(g0, c, ogt, t_ogt)
                            if pend_out:
                                out_finish(*pend_out.pop(0))
                            pend_out.append(st_)

                    def out_chunk(g0, c, ogt, t_ogt):
                        if True:
                            gci = cb0 + g0 + c
                            xres, t_xres = xr_r.next()
                            S.dma("sp", xres[:], x1_hbm[gci * 128:(gci + 1) * 128, :], reads=[t_x1[gci]], writes=[t_xres])
                            x2, t_x2 = x2_r.next()
                            for half in range(2):
                                bank, t_bank = pB.next()
                                cs = slice(half * 512, (half + 1) * 512)
                                for hh in range(8):
                                    S.op("pe", lambda e, hh=hh, cs=cs, bank=bank, c=c: e.matmul(out=bank[:], lhsT=ogt[:, hh, c * 128:(c + 1) * 128], rhs=wo1[:, hh, cs],
                                                                                              start=(hh == 0), stop=(hh == 7)), reads=[t_ogt, t_wo1], writes=[t_bank])
                                S.op("dve", lambda e, cs=cs, bank=bank, x2=x2: e.tensor_tensor(out=x2[:, cs], in0=bank[:], in1=Gt[:, cs], op=ALU.mult),
                                     reads=[t_bank, t_G], writes=[t_x2])
                            S.op("pool", lambda e, x2=x2, xres=xres: e.tensor_tensor(out=x2[:], in0=x2[:], in1=xres[:], op=ALU.add), reads=[t_x2, t_xres], writes=[t_x2])
                            S.op("act", lambda e, x2=x2, gci=gci: e.activation(out=junk1[:], in_=x2[:], func=AF.Square, accum_out=ssq2[:, gci:gci + 1]),
                                 reads=[t_x2], writes=[t_junk1, t_r2[gci]])
                            S.op("act", lambda e, gci=gci: e.activation(out=ln2[:, gci:gci + 1], in_=ssq2[:, gci:gci + 1], func=AF.Ln, scale=1.0 / D, bias=eps_t[:]),
                                 reads=[t_r2[gci], t_eps], writes=[t_r2[gci]])
                            S.op("act", lambda e, gci=gci: e.activation(out=rstd2[:, gci:gci + 1], in_=ln2[:, gci:gci + 1], func=AF.Exp, scale=-0.5),
                                 reads=[t_r2[gci]], writes=[t_r2[gci]])
                            return (gci, x2, t_x2)

                    def out_finish(gci, x2, t_x2):
                        if True:
                            yo, t_yo = yo_r.next()
                            S.op("dve", lambda e, x2=x2, yo=yo, gci=gci: e.scalar_tensor_tensor(out=yo[:], in0=x2[:], scalar=rstd2[:, gci:gci + 1], in1=fng_bc[:],
                                                                                               op0=ALU.mult, op1=ALU.mult), reads=[t_x2, t_r2[gci], t_c1], writes=[t_yo])
                            S.dma("act", y_all[gci * 128:(gci + 1) * 128, :], yo[:], reads=[t_yo], is_output=True)

                    pend_out = []
                    for g0 in range(0, nseq * NC, GWc):
                        out_group(g0)
                    while pend_out:
                        out_finish(*pend_out.pop(0))
                    S.barrier()

            if run_sample:
                l1_sequence(0, T_S // 128, 0, 1)
            if n_prompts > 0:
                l1_sequence(T_S // 128, T_P // 128, 1, n_prompts)
            S.barrier()

        S.finish()
        block = st.enter_context(nc.Block())
        S.emit(block)
    return nc


def prep_inputs(inp, cores=range(N_CORES)):
    f = lambda a: np.ascontiguousarray(np.asarray(a, dtype=np.float32))
    tab, _, _ = l0_tables()
    shared = {
        "ada_w": f(inp["ada_w"]),
        "ada_b_pp": f(np.asarray(inp["ada_b"]).reshape(2, 24, 128).transpose(2, 0, 1)),
        "ada_b_row": f(inp["ada_b"]),
        "ng_pp": f(np.asarray(inp["norm_g"]).reshape(2, 8, 128).transpose(2, 0, 1)),
        "w_in0": f(inp["even_in_w"][0]),
        "convw_pp": f(np.asarray(inp["even_conv_w"])[0].reshape(3, 4, 128).transpose(2, 1, 0)),
        "w_out0": f(inp["even_out_w"][0]),
        "ident": np.eye(128, dtype=np.float32),
        "l0tab": tab,
    }
    perm = np.concatenate([np.arange(16, 32), np.arange(0, 16), np.arange(48, 64), np.arange(32, 48)])
    w1 = np.asarray(inp["odd_in_w"])[0]
    shared["w_in1e"] = f(np.concatenate([w1, w1[:, 640 + perm]], axis=1))
    qu = np.asarray(inp["odd_q_up_w"])[0]
    parts = []
    for h in range(8):
        b = h * 192
        parts += [qu[:, b:b + 128], qu[:, b + 128:b + 192], qu[:, b + 128 + perm]]
    shared["q_up_e"] = f(np.concatenate(parts, axis=1))
    shared["kv_up"] = f(np.asarray(inp["odd_kv_up_w"])[0])
    shared["w_out1"] = f(np.asarray(inp["odd_out_w"])[0])
    shared["qng_pp"] = f(np.asarray(inp["odd_q_norm_g"])[0].reshape(3, 128).T)
    shared["kvng_pp"] = f(np.asarray(inp["odd_kv_norm_g"])[0].reshape(2, 128).T)
    shared["kvng_row"] = f(np.asarray(inp["odd_kv_norm_g"])[0].reshape(1, 256))
    shared["fng_row"] = f(np.asarray(inp["final_norm_g"]).reshape(1, D))
    shared["ropetab"] = rope_table()
    maps = []
    xs = np.asarray(inp["x_sample"])
    xp = np.asarray(inp["x_prompt"])
    c = np.asarray(inp["c"])
    cc = np.asarray(inp["c_ctx"])
    for r in cores:
        m = dict(shared)
        m["x_all"] = f(np.concatenate([xs[r], xp[N_P * r:N_P * (r + 1)].reshape(N_P * T_P, D)], axis=0))
        m["cvec"] = f(np.stack([c[r].reshape(8, 128).T, cc.reshape(8, 128).T], axis=-1))
        m["st_f"] = f(inp["state_ret_fwd"][r, 0])
        m["st_b"] = f(inp["state_ret_bwd"][r, 0])
        m["c_ckv"] = f(inp["cache_mla_ckv"][r, 0])
        m["c_kr"] = f(inp["cache_mla_krope"][r, 0])
        maps.append(m)
    return maps


_NC_CACHE = {}


def kernel(**inputs):
    if "nc" not in _NC_CACHE:
        _NC_CACHE["nc"] = build_program()
    nc = _NC_CACHE["nc"]
    maps = prep_inputs(inputs)
    res = run_bass_kernel_spmd(nc, maps, core_ids=list(range(N_CORES)))
    outs = res.results
    y_prompt = np.zeros((N_CORES * N_P, T_P, D), np.float32)
    y_sample = np.zeros((N_CORES, T_S, D), np.float32)
    nsf = np.zeros((N_CORES * N_P, 1, 4, 128, 128), np.float32)
    nsb = np.zeros((N_CORES * N_P, 1, 4, 128, 128), np.float32)
    nckv = np.zeros((N_CORES * N_P, 1, T_P, 256), np.float32)
    nkr = np.zeros((N_CORES * N_P, 1, T_P, 64), np.float32)
    for r in range(N_CORES):
        o = outs[r]
        y_sample[r] = o["y_all"][:T_S]
        y_prompt[N_P * r:N_P * (r + 1)] = o["y_all"][T_S:].reshape(N_P, T_P, D)
        nsf[N_P * r:N_P * (r + 1), 0] = o["new_sf"]
        nsb[N_P * r:N_P * (r + 1), 0] = o["new_sb"]
        nckv[N_P * r:N_P * (r + 1), 0] = o["new_ckv"]
        nkr[N_P * r:N_P * (r + 1), 0] = o["new_kr"]
    return (y_prompt, y_sample, nsf, nsb, nckv, nkr)
```

```python
import contextlib
import numpy as np
import concourse.bass as bass
import concourse.mybir as mybir
from concourse.bass_utils import run_bass_kernel_spmd

F32 = mybir.dt.float32
BF16 = mybir.dt.bfloat16
AF = mybir.ActivationFunctionType
ALU = mybir.AluOpType

ENG_NAMES = ("pe", "act", "dve", "pool", "sp")
DMA_RING = 12
EPS = 1e-6

N_CORES = 8
D = 1024
T_S = 4096
T_P = 256
N_P = 4
ROWS = T_S + N_P * T_P
NCH = ROWS // 128
PAST = 512


class T:
    __slots__ = ("name", "w", "r", "excl")

    def __init__(self, name="", excl=False):
        self.name = name
        self.w = None
        self.r = {}
        self.excl = excl


class Sched:
    def __init__(self, nc, stack):
        self.nc = nc
        self.sem = {e: stack.enter_context(nc.semaphore("s_" + e)) for e in ENG_NAMES}
        self.cnt = {e: 0 for e in ENG_NAMES}
        self.prog = {e: [] for e in ENG_NAMES}
        self.seen = {e: {} for e in ENG_NAMES}
        self.dsem = {}
        self.dcnt = {}
        for q in ("sp", "pool", "act"):
            self.dsem[q] = [stack.enter_context(nc.semaphore("d_%s%d" % (q, i)))
                            for i in range(DMA_RING)]
            self.dcnt[q] = 0
        self.out_events = []

    def _deps(self, eng, reads, writes):
        best = {}

        def add(key, v):
            if v > best.get(key, 0):
                best[key] = v

        for t in reads:
            w = t.w
            if w is not None and not (w[0] == ("e", "pe") and eng == "pe"):
                add(w[0], w[1])
            if t.excl:
                for key, v in t.r.items():
                    if key != ("e", eng):
                        add(key, v)
        for t in writes:
            w = t.w
            if w is not None and w[0] != ("e", eng):
                add(w[0], w[1])
            for key, v in t.r.items():
                if key != ("e", eng):
                    add(key, v)
        waits = []
        seen = self.seen[eng]
        for key, v in best.items():
            if seen.get(key, 0) >= v:
                continue
            seen[key] = v
            waits.append((key, v))
        return waits

    def _semof(self, key):
        if key[0] == "e":
            return self.sem[key[1]]
        q, i = key[1]
        return self.dsem[q][i]

    def _mark(self, key, val, reads, writes):
        for t in reads:
            if t.r.get(key, 0) < val:
                t.r[key] = val
        for t in writes:
            t.w = (key, val)
            t.r = {}

    def op(self, eng, fn, reads=(), writes=()):
        waits = self._deps(eng, reads, writes)
        self.cnt[eng] += 1
        self.prog[eng].append(("op", waits, fn, None))
        self._mark(("e", eng), self.cnt[eng], reads, writes)

    def dma(self, q, out, in_, reads=(), writes=(), is_output=False, **kw):
        i = self.dcnt[q] % DMA_RING
        gen = self.dcnt[q] // DMA_RING
        self.dcnt[q] += 1
        waits = self._deps(q, reads, writes)
        key = ("d", (q, i))
        if gen > 0:
            prev = 16 * gen
            if self.seen[q].get(key, 0) < prev:
                self.seen[q][key] = prev
                waits.append((key, prev))
        val = 16 * (gen + 1)
        self.prog[q].append(("dma", waits, (out, in_, kw), self.dsem[q][i]))
        self._mark(key, val, reads, writes)
        if is_output:
            self.out_events.append((key, val))

    def barrier(self):
        tgt = []
        for e in ENG_NAMES:
            if self.cnt[e] > 0:
                tgt.append((("e", e), self.cnt[e]))
        for q in self.dsem:
            n = self.dcnt[q]
            for i in range(DMA_RING):
                k = (n - i + DMA_RING - 1) // DMA_RING
                if k > 0:
                    tgt.append((("d", (q, i)), 16 * k))
        for e in ENG_NAMES:
            waits = []
            for key, v in tgt:
                if key == ("e", e) and e != "sp":
                    pass
                if self.seen[e].get(key, 0) < v:
                    self.seen[e][key] = v
                    waits.append((key, v))
            self.prog[e].append(("waitonly", waits, None, None))

    def finish(self):
        best = {}
        for key, v in self.out_events:
            best[key] = max(best.get(key, 0), v)
        self.prog["sp"].append(("waitonly", list(best.items()), None, None))

    def emit(self, block):
        S = self
        nc = self.nc
        engs = {"pe": nc.tensor, "act": nc.scalar, "dve": nc.vector, "pool": nc.gpsimd, "sp": nc.sync}

        def run(ename, engobj):
            own = S.sem[ename]
            for kind, waits, payload, inc in S.prog[ename]:
                for key, v in waits:
                    engobj.wait_ge(S._semof(key), v)
                if kind == "op":
                    payload(engobj).then_inc(own, 1)
                elif kind == "dma":
                    out, in_, kw = payload
                    engobj.dma_start(out=out, in_=in_, **kw).then_inc(inc, 16)

        @block.tensor
        def _(e):
            run("pe", e)

        @block.scalar
        def _(e):
            run("act", e)

        @block.vector
        def _(e):
            run("dve", e)

        @block.gpsimd
        def _(e):
            run("pool", e)

        @block.sync
        def _(e):
            run("sp", e)


class Ring:
    def __init__(self, bufs, excl=False):
        self.bufs = bufs
        self.ts = [T(excl=excl) for _ in bufs]
        self.i = -1

    def next(self):
        self.i = (self.i + 1) % len(self.bufs)
        return self.bufs[self.i], self.ts[self.i]

    def cur(self):
        return self.bufs[self.i], self.ts[self.i]

    def sub(self, idx):
        r = Ring([self.bufs[i] for i in idx])
        r.ts = [self.ts[i] for i in idx]
        return r


def l0_tables():
    C = 128
    scale = 128.0 ** -0.5
    tab = np.zeros((128, 5, 4, 128), np.float64)
    cF, cB = [], []
    i = np.arange(C, dtype=np.float64)
    for h in range(4):
        gf = 1.0 - 2.0 ** (-5.0 - h)
        gb = 1.0 - 2.0 ** (-5.5 - h)
        cF.append(float(np.float32(gf ** C)))
        cB.append(float(np.float32(gb ** C)))
        ii = i[None, :]
        jj = i[:, None]
        m = np.where(ii > jj, gf ** np.maximum(ii - jj, 0), np.where(jj > ii, gb ** np.maximum(jj - ii, 0), 2.0))
        tab[:, 0, h, :] = scale * m
        tab[:, 1, h, :] = (gf ** (i + 1.0))[None, :]
        tab[:, 2, h, :] = (gb ** (C - i))[None, :]
        tab[:, 3, h, :] = (scale * gf ** (C - 1.0 - i))[:, None]
        tab[:, 4, h, :] = (scale * gb ** i)[:, None]
    return tab.reshape(128, 5, 512).astype(np.float32), cF, cB


def rope_table():
    f = 16
    inv = (np.float32(10000.0) ** (-np.arange(f, dtype=np.float32) / np.float32(f))).astype(np.float32)
    idx = np.arange(64, dtype=np.float32)
    tab = np.zeros((64, 2, 64), np.float32)
    for p in range(64):
        ang = (idx * inv[p % 16]).astype(np.float32)
        sign = -1.0 if (p % 32) < 16 else 1.0
        tab[p, 0] = np.cos(ang)
        tab[p, 1] = sign * np.sin(ang)
    return tab


def build_program(dbg=False, do_l1=True, run_sample=True, n_prompts=N_P, do_l0=True):
    nc = bass.Bass("TRN2", target_bir_lowering=False)
    _, cF, cB = l0_tables()

    def din(name, shape, dt=F32):
        return nc.dram_tensor(name, list(shape), dt, kind="ExternalInput").ap()

    def dout(name, shape, dt=F32):
        return nc.dram_tensor(name, list(shape), dt, kind="ExternalOutput").ap()

    def dscr(name, shape, dt=F32):
        return nc.dram_tensor(name, list(shape), dt, kind="Internal").ap()

    x_all = din("x_all", [ROWS, D])
    cvec = din("cvec", [128, 8, 2])
    st_f = din("st_f", [4, 128, 128])
    st_b = din("st_b", [4, 128, 128])
    ada_w = din("ada_w", [2, D, 3 * D])
    ada_b_pp = din("ada_b_pp", [128, 2, 24])
    ada_b_row = din("ada_b_row", [2, 3 * D])
    ng_pp = din("ng_pp", [128, 2, 8])
    w_in0 = din("w_in0", [D, 4096])
    convw_pp = din("convw_pp", [128, 4, 3])
    w_out0 = din("w_out0", [D, D])
    ident_d = din("ident", [128, 128])
    l0tab_d = din("l0tab", [128, 5, 512])

    w_in1e = din("w_in1e", [D, 1792])
    q_up_e = din("q_up_e", [384, 2048])
    kv_up_d = din("kv_up", [256, 2048])
    w_out1_d = din("w_out1", [D, D])
    qng_pp = din("qng_pp", [128, 3])
    kvng_pp = din("kvng_pp", [128, 2])
    kvng_row = din("kvng_row", [1, 256])
    fng_row = din("fng_row", [1, D])
    ropetab = din("ropetab", [64, 2, 64])
    c_ckv = din("c_ckv", [PAST, 256])
    c_kr = din("c_kr", [PAST, 64])
    gsc = dscr("gsc", [8, 128, ROWS], BF16)
    osc = dscr("osc", [8, 128, ROWS], BF16)
    new_ckv = dout("new_ckv", [N_P, T_P, 256])
    new_kr = dout("new_kr", [N_P, T_P, 64])

    y_all = dout("y_all", [ROWS, D])
    new_sf = dout("new_sf", [N_P, 4, 128, 128])
    new_sb = dout("new_sb", [N_P, 4, 128, 128])
    if dbg:
        x1_hbm = dout("x1_dbg", [ROWS, D])
    else:
        x1_hbm = dscr("x1_scr", [ROWS, D])
    snap_hbm = dscr("snap_scr", [NCH, 128, 512], BF16)
    g_hbm = dscr("g_scr", [2, 2, D])

    with contextlib.ExitStack() as st:
        S = Sched(nc, st)

        uid = [0]

        def sb(stack, name, shape, dt):
            uid[0] += 1
            return stack.enter_context(nc.sbuf_tensor("sb_%s_%d" % (name, uid[0]), list(shape), dt))

        def ps(stack, name, shape, dt):
            return stack.enter_context(nc.psum_tensor("ps_" + name, list(shape), dt))

        pT = Ring([ps(st, "pT%d" % i, [128, 8, 128], BF16) for i in range(2)], excl=True)
        pB = Ring([ps(st, "pB%d" % i, [128, 512], F32) for i in range(6)], excl=True)

        ident_bf = sb(st, "ident_bf", [128, 128], BF16)
        ones_bf = sb(st, "ones_bf", [128, 128], BF16)
        ones_f32 = sb(st, "ones_f32", [128, 128], F32)
        t_const = T("const")
        A_pp = sb(st, "A_pp", [128, 2, 2, 8], F32)
        B_pp = sb(st, "B_pp", [128, 2, 2, 8], F32)
        t_AB = T("AB")
        ssq0 = sb(st, "ssq0", [128, NCH], F32)
        rstd0 = sb(st, "rstd0", [128, NCH], F32)
        ssq1 = sb(st, "ssq1", [128, NCH], F32)
        rstd1 = sb(st, "rstd1", [128, NCH], F32)
        t_r0 = [T() for _ in range(NCH)]
        t_r1 = [T() for _ in range(NCH)]
        t_x1 = [T() for _ in range(NCH)]
        t_snap = [T() for _ in range(NCH)]
        t_g = T("g_hbm")

        eps_t = sb(st, "eps_t", [128, 1], F32)
        one_t = sb(st, "one_t", [128, 1], F32)
        t_eps = T()
        S.op("pool", lambda e: e.memset(eps_t[:], EPS), writes=[t_eps])
        S.op("pool", lambda e: e.memset(one_t[:], 1.0), writes=[t_eps])
        S.dma("pool", ident_bf[:], ident_d, writes=[t_const])
        S.op("pool", lambda e: e.memset(ones_bf[:], 1.0), writes=[t_const])
        S.op("pool", lambda e: e.memset(ones_f32[:], 1.0), writes=[t_const])

        with contextlib.ExitStack() as sa:
            cv = sb(sa, "cv", [128, 16], F32)
            cve = sb(sa, "cve", [128, 16], F32)
            sc_bf = sb(sa, "sc_bf", [128, 8, 2], BF16)
            adab = sb(sa, "adab", [128, 2, 24], F32)
            ngp = sb(sa, "ngp", [128, 2, 8], F32)
            brow = sb(sa, "brow", [2, 2, 3 * D], F32)
            grow = sb(sa, "grow", [2, 2, D], F32)
            mpp = sb(sa, "mpp", [128, 2, 16, 2], F32)
            aw = Ring([sb(sa, "aw%d" % i, [128, 8, 512], BF16) for i in range(2)])
            t_cv, t_sc, t_misc, t_grow, t_mpp = T(), T(), T(), T(), T()
            S.dma("sp", cv[:], cvec.rearrange("p c g -> p (c g)"), writes=[t_cv])
            S.dma("sp", adab[:], ada_b_pp, writes=[t_misc])
            S.dma("sp", ngp[:], ng_pp, writes=[t_misc])
            S.dma("sp", brow[0:1, :, :], ada_b_row.rearrange("(o l) n -> o l n", o=1), writes=[t_misc])
            S.dma("sp", brow[1:2, :, :], ada_b_row.rearrange("(o l) n -> o l n", o=1), writes=[t_misc])
            S.op("act", lambda e: e.activation(out=cve[:], in_=cv[:], func=AF.Exp, scale=-1.0), reads=[t_cv], writes=[t_sc])
            S.op("dve", lambda e: e.tensor_scalar(out=cve[:], in0=cve[:], scalar1=1.0, scalar2=None, op0=ALU.add), reads=[t_sc], writes=[t_sc])
            S.op("dve", lambda e: e.reciprocal(out=cve[:], in_=cve[:]), reads=[t_sc], writes=[t_sc])
            S.op("dve", lambda e: e.tensor_tensor(out=sc_bf[:].rearrange("p c g -> p (c g)"), in0=cv[:], in1=cve[:], op=ALU.mult),
                 reads=[t_sc, t_cv], writes=[t_sc])
            for l in range(2):
                for cb in range(6):
                    awt, t_aw = aw.next()
                    S.dma("pool", awt[:], ada_w[l, :, cb * 512:(cb + 1) * 512].rearrange("(c p) n -> p c n", p=128), writes=[t_aw])
                    if cb < 4:
                        bank, t_bank = pB.next()
                        for j in range(4):
                            for fc in range(8):
                                S.op("pe", lambda e, j=j, fc=fc, bank=bank, awt=awt: e.matmul(
                                    out=bank[:, j * 2:j * 2 + 2], lhsT=awt[:, fc, j * 128:(j + 1) * 128], rhs=sc_bf[:, fc, :],
                                    start=(fc == 0), stop=(fc == 7)), reads=[t_aw, t_sc], writes=[t_bank])
                        S.op("dve", lambda e, bank=bank, l=l, cb=cb: e.tensor_copy(
                            out=mpp[:, l, cb * 4:(cb + 1) * 4, :], in_=bank[:, 0:8].rearrange("p (j g) -> p j g", g=2)),
                            reads=[t_bank], writes=[t_mpp])
                    else:
                        bank, t_bank = pB.next()
                        for fc in range(8):
                            S.op("pe", lambda e, fc=fc, bank=bank, awt=awt: e.matmul(
                                out=bank[0:2, :], lhsT=sc_bf[:, fc, :], rhs=awt[:, fc, :], start=(fc == 0), stop=(fc == 7)),
                                reads=[t_aw, t_sc], writes=[t_bank])
                        S.op("dve", lambda e, bank=bank, l=l, cb=cb: e.tensor_tensor(
                            out=grow[:, l, (cb - 4) * 512:(cb - 3) * 512], in0=bank[0:2, :],
                            in1=brow[:, l, cb * 512:(cb + 1) * 512], op=ALU.add), reads=[t_bank, t_misc], writes=[t_grow])
            for l in range(2):
                for g in range(2):
                    S.op("dve", lambda e, l=l, g=g: e.tensor_tensor(out=B_pp[:, l, g, :], in0=mpp[:, l, 0:8, g], in1=adab[:, l, 0:8], op=ALU.add),
                         reads=[t_mpp, t_misc], writes=[t_AB])
                    S.op("dve", lambda e, l=l, g=g: e.tensor_tensor(out=A_pp[:, l, g, :], in0=mpp[:, l, 8:16, g], in1=adab[:, l, 8:16], op=ALU.add),
                         reads=[t_mpp, t_misc], writes=[t_AB])
                    S.op("dve", lambda e, l=l, g=g: e.scalar_tensor_tensor(out=A_pp[:, l, g, :], in0=A_pp[:, l, g, :], scalar=1.0, in1=ngp[:, l, :],
                                                                          op0=ALU.add, op1=ALU.mult), reads=[t_AB, t_misc], writes=[t_AB])
            S.dma("pool", g_hbm.rearrange("l g n -> g l n"), grow[:], reads=[t_grow], writes=[t_g])
            S.barrier()

        with contextlib.ExitStack() as s0:
          if do_l0:
              w_in = sb(s0, "w_in", [128, 8, 4096], BF16)
              w_out = sb(s0, "w_out", [128, 8, 1024], BF16)
              t_win, t_wout = T("w_in"), T("w_out")
              t_wkv = T("w_in_kv")
              for cb in (1, 2, 0, 3, 4, 5, 6, 7):
                  S.dma("pool", w_in[:, :, cb * 512:(cb + 1) * 512],
                        w_in0[:, cb * 512:(cb + 1) * 512].rearrange("(c p) n -> p c n", p=128), writes=[t_wkv if cb in (1, 2) else t_win])
              for cb in range(2):
                  S.dma("pool", w_out[:, :, cb * 512:(cb + 1) * 512],
                        w_out0[:, cb * 512:(cb + 1) * 512].rearrange("(c p) n -> p c n", p=128), writes=[t_wout])
              l0tab = sb(s0, "l0tab", [128, 5, 512], F32)
              cw = sb(s0, "cw", [128, 4, 3], F32)
              t_tab = T("l0tab")
              S.dma("sp", l0tab[:], l0tab_d, writes=[t_tab])
              S.dma("sp", cw[:], convw_pp, writes=[t_tab])
              maskT, qdecF, qdecB, kdecF, kdecB = (l0tab[:, i, :] for i in range(5))
              G_bc = Ring([sb(s0, "G_bc%d" % i, [128, 1024], F32) for i in range(2)])

              def ring(name, n, shape, dt):
                  return Ring([sb(s0, "%s%d" % (name, i), shape, dt) for i in range(n)])

              xt_r = ring("xt", 4, [128, 1024], F32)
              junk = sb(s0, "junk", [128, 1024], BF16)
              t_junk = T()
              lnt = sb(s0, "lnt", [128, NCH], F32)
              xn_r = ring("xn", 3, [128, 1024], BF16)
              hT_r = ring("hT", 4, [128, 8, 128], BF16)
              qT_r = ring("qT", 2, [128, 512], BF16)
              qTf_r = ring("qTf", 2, [128, 512], BF16)
              qTb_r = ring("qTb", 2, [128, 512], BF16)
              kT_r = ring("kT", 2, [128, 512], BF16)
              sga_r = ring("sga", 2, [128, 512], F32)
              sgb_r = ring("sgb", 1, [128, 512], F32)
              et_r = ring("et", 2, [128, 512], F32)
              cgs_r = ring("cgs", 1, [128, 512], F32)
              u_r = ring("u", 2, [128, 512], F32)
              z_r = ring("z", 2, [128, 4, 130], F32)
              kf_r = ring("kf", 2, [128, 512], BF16)
              vb_r = ring("vb", 2, [128, 512], BF16)
              attm_r = ring("attm", 2, [128, 512], BF16)
              snapr_r = ring("snapr", 2, [128, 512], BF16)
              snapw_r = ring("snapw", 2, [128, 512], BF16)
              osb_r = ring("osb", 1, [128, 512], F32)
              osq_r = ring("osq", 1, [128, 512], BF16)
              ms_r = ring("ms", 1, [128, 512], F32)
              yat_r = ring("yat", 1, [128, 512], F32)
              ymix_r = ring("ymix", 2, [128, 8, 128], BF16)
              zc_r = ring("zc", 1, [128, 4, 128], F32)
              xres_r = ring("xres", 2, [128, 1024], F32)
              x1t_r = ring("x1t", 2, [128, 1024], F32)
              S_f = sb(s0, "S_f", [128, 512], F32)
              S_b = sb(s0, "S_b", [128, 512], F32)
              Sf_bf = sb(s0, "Sf_bf", [128, 512], BF16)
              t_Sf, t_Sb, t_Sfbf = T(), T(), T()

              def silu_psum(bank, t_bank, out_ap, t_out):
                  et, t_et = et_r.next()
                  S.op("act", lambda e: e.activation(out=et[:], in_=bank[:], func=AF.Exp, scale=-1.0), reads=[t_bank], writes=[t_et])
                  S.op("act", lambda e: e.activation(out=et[:], in_=et[:], func=AF.Ln, bias=one_t[:]), reads=[t_et, t_eps], writes=[t_et])
                  S.op("act", lambda e: e.activation(out=et[:], in_=et[:], func=AF.Exp, scale=-1.0), reads=[t_et], writes=[t_et])
                  S.op("dve", lambda e: e.tensor_tensor(out=out_ap, in0=bank[:], in1=et[:], op=ALU.mult), reads=[t_bank, t_et], writes=[t_out])

              def rstd_from_ssq(ssq_col, ln_col, out_col, n, t_col):
                  S.op("act", lambda e: e.activation(out=ln_col, in_=ssq_col, func=AF.Ln, scale=1.0 / n, bias=eps_t[:]),
                       reads=[t_col, t_eps], writes=[t_col])
                  S.op("act", lambda e: e.activation(out=out_col, in_=ln_col, func=AF.Exp, scale=-0.5), reads=[t_col], writes=[t_col])


              def fe_load(gci, stats):
                  xt, t_xt = xt_r.next()
                  S.dma("sp", xt[:], x_all[gci * 128:(gci + 1) * 128, :], writes=[t_xt])
                  if stats:
                      S.op("act", lambda e: e.activation(out=junk[:], in_=xt[:], func=AF.Square, accum_out=ssq0[:, gci:gci + 1]),
                           reads=[t_xt], writes=[t_junk, t_r0[gci]])
                      rstd_from_ssq(ssq0[:, gci:gci + 1], lnt[:, gci:gci + 1], rstd0[:, gci:gci + 1], float(D), t_r0[gci])
                  return xt, t_xt

              def frontend(gci, grp, stats):
                  return fe_main(gci, grp, *fe_load(gci, stats))

              def fe_main(gci, grp, xt, t_xt):
                  return fe_T(gci, grp, *fe_norm(gci, xt, t_xt))

              def fe_norm(gci, xt, t_xt):
                  xn, t_xn = xn_r.next()
                  S.op("dve", lambda e: e.tensor_scalar(out=xn[:], in0=xt[:], scalar1=rstd0[:, gci:gci + 1], scalar2=None, op0=ALU.mult),
                       reads=[t_xt, t_r0[gci]], writes=[t_xn])
                  return xn, t_xn

              def fe_T(gci, grp, xn, t_xn):
                  pt, t_pt = pT.next()
                  for fc in range(8):
                      S.op("pe", lambda e, fc=fc: e.transpose(out=pt[:, fc, :], in_=xn[:, fc * 128:(fc + 1) * 128], identity=ident_bf[:]),
                           reads=[t_xn, t_const], writes=[t_pt])
                  hT, t_hT = hT_r.next()
                  for fc in range(8):
                      if fc % 4 != 3:
                          S.op("act", lambda e, fc=fc: e.activation(out=hT[:, fc, :], in_=pt[:, fc, :], func=AF.Identity,
                                                                    scale=A_pp[:, 0, grp, fc:fc + 1], bias=B_pp[:, 0, grp, fc:fc + 1]),
                               reads=[t_pt, t_AB], writes=[t_hT])
                  for fc in range(8):
                      if fc % 4 == 3:
                          S.op("dve", lambda e, fc=fc: e.tensor_scalar(out=hT[:, fc, :], in0=pt[:, fc, :], scalar1=A_pp[:, 0, grp, fc:fc + 1],
                                                                      scalar2=B_pp[:, 0, grp, fc:fc + 1], op0=ALU.mult, op1=ALU.add),
                               reads=[t_pt, t_AB], writes=[t_hT])
                  return hT, t_hT

              def proj_tok(hT, t_hT, col0):
                  bank, t_bank = pB.next()
                  for fc in range(8):
                      S.op("pe", lambda e, fc=fc: e.matmul(out=bank[:], lhsT=hT[:, fc, :], rhs=w_in[:, fc, col0:col0 + 512],
                                                           start=(fc == 0), stop=(fc == 7)), reads=[t_hT, t_wkv], writes=[t_bank])
                  return bank, t_bank

              def proj_feat(hT, t_hT, col0):
                  bank, t_bank = pB.next()
                  for j in range(4):
                      for fc in range(8):
                          S.op("pe", lambda e, fc=fc, j=j: e.matmul(out=bank[:, j * 128:(j + 1) * 128],
                                                                    lhsT=w_in[:, fc, col0 + j * 128:col0 + (j + 1) * 128], rhs=hT[:, fc, :],
                                                                    start=(fc == 0), stop=(fc == 7)), reads=[t_hT, t_win, t_wkv], writes=[t_bank])
                  return bank, t_bank

              def state_update(Sx, t_Sx, kx, t_kx, vb, t_vb, cdec):
                  bank, t_bank = pB.next()
                  for h in range(4):
                      hs = slice(h * 128, (h + 1) * 128)
                      S.op("pe", lambda e, hs=hs: e.matmul(out=bank[:, hs], lhsT=kx[:, hs], rhs=vb[:, hs], start=True, stop=True),
                           reads=[t_kx, t_vb], writes=[t_bank])
                  for h in range(4):
                      hs = slice(h * 128, (h + 1) * 128)
                      S.op("dve", lambda e, hs=hs, h=h: e.scalar_tensor_tensor(out=Sx[:, hs], in0=Sx[:, hs], scalar=cdec[h], in1=bank[:, hs],
                                                                               op0=ALU.mult, op1=ALU.add), reads=[t_Sx, t_bank], writes=[t_Sx])

              def kv_tok(hT, t_hT, kdec):
                  bank, t_bank = proj_tok(hT, t_hT, 512)
                  kx, t_kx = kf_r.next()
                  S.op("dve", lambda e: e.tensor_tensor(out=kx[:], in0=bank[:], in1=kdec, op=ALU.mult), reads=[t_bank, t_tab], writes=[t_kx])
                  bank2, t_bank2 = proj_tok(hT, t_hT, 1024)
                  vb, t_vb = vb_r.next()
                  S.op("act", lambda e: e.copy(out=vb[:], in_=bank2[:]), reads=[t_bank2], writes=[t_vb])
                  return kx, t_kx, vb, t_vb

              def l0_sequence(cb0, NCs, grp, nseq):
                  NC = NCs * nseq
                  Gt, t_G = G_bc.next()
                  S.dma("sp", Gt[:], g_hbm[0, grp:grp + 1, :].partition_broadcast(128), reads=[t_g], writes=[t_G])
                  if grp == 0:
                      S.dma("sp", S_b[:].rearrange("p (h e) -> p h e", h=4), st_b.rearrange("h d e -> d h e"), writes=[t_Sb])
                      S.dma("sp", S_f[:].rearrange("p (h e) -> p h e", h=4), st_f.rearrange("h d e -> d h e"), writes=[t_Sf])
                  def r_proj(ci):
                      hT, t_hT = fr[ci]
                      return kv_tok(hT, t_hT, kdecB)

                  fr = {}
                  ld = {}
                  for k_ in range(NC - 1, max(NC - 5, -1), -1):
                      ld[k_] = fe_load(cb0 + k_, True)
                  for k_ in range(NC - 1, max(NC - 4, -1), -1):
                      fr[k_] = fe_main(cb0 + k_, grp, *ld.pop(k_))
                  cur = r_proj(NC - 1)
                  for ci in reversed(range(NC)):
                      gci = cb0 + ci
                      nxt_kv = None
                      nrm = None
                      if ci >= 3:
                          nrm = fe_norm(gci - 3, *ld.pop(ci - 3))
                      if ci >= 1:
                          nxt_kv = r_proj(ci - 1)
                      kx, t_kx, vb, t_vb = cur
                      if grp == 1 and ci % NCs == NCs - 1:
                          S.op("pool", lambda e: e.memset(S_b[:], 0.0), writes=[t_Sb])
                      sw, t_sw = snapw_r.next()
                      S.op("act", lambda e, sw=sw: e.copy(out=sw[:], in_=S_b[:]), reads=[t_Sb], writes=[t_sw])
                      S.dma("pool", snap_hbm[gci], sw[:], reads=[t_sw], writes=[t_snap[gci]])
                      state_update(S_b, t_Sb, kx, t_kx, vb, t_vb, cB)
                      if ci >= 3:
                          fr[ci - 3] = fe_T(gci - 3, grp, *nrm)
                      if ci >= 4:
                          ld[ci - 4] = fe_load(gci - 4, True)
                      cur = nxt_kv
                      fr.pop(ci, None)
                      if grp == 1 and ci % NCs == 0:
                          S.dma("pool", new_sb[ci // NCs].rearrange("h d e -> d h e"), S_b[:].rearrange("p (h e) -> p h e", h=4),
                                reads=[t_Sb], is_output=True)
                  if grp == 0:
                      S.op("act", lambda e: e.copy(out=Sf_bf[:], in_=S_f[:]), reads=[t_Sf], writes=[t_Sfbf])
                  def stageA(ci):
                      hT, t_hT = fr[ci]
                      c = {}

                      def a0():
                          bank, t_bank = proj_feat(hT, t_hT, 2560)
                          cgs, t_cgs = cgs_r.next()
                          S.op("act", lambda e: e.copy(out=cgs[:], in_=bank[:]), reads=[t_bank], writes=[t_cgs])
                          bank2, t_bank2 = proj_feat(hT, t_hT, 3072)
                          z, t_z = z_r.next()
                          S.op("dve", lambda e: e.tensor_tensor(
                              out=z[:, :, 1:129], in0=bank2[:].rearrange("p (f t) -> p f t", f=4), in1=cgs[:].rearrange("p (f t) -> p f t", f=4), op=ALU.mult),
                              reads=[t_bank2, t_cgs], writes=[t_z])
                          if ci % NCs == 0:
                              S.op("pool", lambda e: e.memset(z[:, :, 0:1], 0.0), writes=[t_z])
                          c["z"] = (z, t_z)

                      def a1():
                          bank, t_bank = proj_feat(hT, t_hT, 0)
                          qT, t_qT = qT_r.next()
                          qTf, t_qTf = qTf_r.next()
                          qTb, t_qTb = qTb_r.next()
                          S.op("act", lambda e: e.copy(out=qT[:], in_=bank[:]), reads=[t_bank], writes=[t_qT])
                          S.op("dve", lambda e: e.tensor_tensor(out=qTf[:], in0=bank[:], in1=qdecF, op=ALU.mult), reads=[t_bank, t_tab], writes=[t_qTf])
                          S.op("dve", lambda e: e.tensor_tensor(out=qTb[:], in0=bank[:], in1=qdecB, op=ALU.mult), reads=[t_bank, t_tab], writes=[t_qTb])
                          bank2, t_bank2 = proj_feat(hT, t_hT, 512)
                          kT, t_kT = kT_r.next()
                          S.op("act", lambda e: e.copy(out=kT[:], in_=bank2[:]), reads=[t_bank2], writes=[t_kT])
                          c.update(qT=(qT, t_qT), qTf=(qTf, t_qTf), qTb=(qTb, t_qTb), kT=(kT, t_kT))

                      def a2():
                          bank, t_bank = proj_feat(hT, t_hT, 1536)
                          sga, t_sga = sga_r.next()
                          silu_psum(bank, t_bank, sga[:], t_sga)
                          bank2, t_bank2 = proj_feat(hT, t_hT, 3584)
                          sgb, t_sgb = sgb_r.next()
                          silu_psum(bank2, t_bank2, sgb[:], t_sgb)
                          c.update(sga=(sga, t_sga), sgb=(sgb, t_sgb))

                      def a3():
                          bank, t_bank = proj_feat(hT, t_hT, 2048)
                          u, t_u = u_r.next()
                          sgb, t_sgb = c["sgb"]
                          S.op("dve", lambda e: e.tensor_tensor(out=u[:], in0=bank[:], in1=sgb[:], op=ALU.mult), reads=[t_bank, t_sgb], writes=[t_u])
                          c["u"] = (u, t_u)
                          c["kv"] = kv_tok(hT, t_hT, kdecF)

                      return c, [a0, a1, a2, a3]

                  def stageB(ci, c):
                      gci = cb0 + ci
                      qT, t_qT = c["qT"]
                      qTf, t_qTf = c["qTf"]
                      qTb, t_qTb = c["qTb"]
                      kT, t_kT = c["kT"]
                      sga, t_sga = c["sga"]
                      kx, t_kx, vb, t_vb = c["kv"]
                      d = {}

                      def b0():
                          bank, t_bank = pB.next()
                          for h in range(4):
                              hs = slice(h * 128, (h + 1) * 128)
                              S.op("pe", lambda e, hs=hs: e.matmul(out=bank[:, hs], lhsT=kT[:, hs], rhs=qT[:, hs], start=True, stop=True),
                                   reads=[t_kT, t_qT], writes=[t_bank])
                          attm, t_attm = attm_r.next()
                          S.op("dve", lambda e: e.tensor_tensor(out=attm[:], in0=bank[:], in1=maskT, op=ALU.mult), reads=[t_bank, t_tab], writes=[t_attm])
                          snr, t_snr = snapr_r.next()
                          S.dma("sp", snr[:], snap_hbm[gci], reads=[t_snap[gci]], writes=[t_snr])
                          d.update(attm=(attm, t_attm), snr=(snr, t_snr))

                      def b1():
                          attm, t_attm = d["attm"]
                          snr, t_snr = d["snr"]
                          bank, t_bank = pB.next()
                          for h in range(4):
                              hs = slice(h * 128, (h + 1) * 128)
                              S.op("pe", lambda e, hs=hs: e.matmul(out=bank[:, hs], lhsT=vb[:, hs], rhs=attm[:, hs], start=True, stop=False),
                                   reads=[t_vb, t_attm], writes=[t_bank])
                              S.op("pe", lambda e, hs=hs: e.matmul(out=bank[:, hs], lhsT=Sf_bf[:, hs], rhs=qTf[:, hs], start=False, stop=False),
                                   reads=[t_Sfbf, t_qTf], writes=[t_bank])
                              S.op("pe", lambda e, hs=hs: e.matmul(out=bank[:, hs], lhsT=snr[:, hs], rhs=qTb[:, hs], start=False, stop=True),
                                   reads=[t_snr, t_qTb], writes=[t_bank])
                          osb, t_osb = osb_r.next()
                          osq, t_osq = osq_r.next()
                          S.op("act", lambda e: e.copy(out=osb[:], in_=bank[:]), reads=[t_bank], writes=[t_osb])
                          S.op("act", lambda e: e.activation(out=osq[:], in_=bank[:], func=AF.Square), reads=[t_bank], writes=[t_osq])
                          d.update(osb=(osb, t_osb), osq=(osq, t_osq))

                      def b2():
                          osb, t_osb = d["osb"]
                          osq, t_osq = d["osq"]
                          bank, t_bank = pB.next()
                          S.op("pe", lambda e: e.matmul(out=bank[:], lhsT=ones_bf[:], rhs=osq[:], start=True, stop=True), reads=[t_osq, t_const], writes=[t_bank])
                          ms, t_ms = ms_r.next()
                          S.op("act", lambda e: e.activation(out=ms[:], in_=bank[:], func=AF.Ln, scale=1.0 / 128, bias=eps_t[:]), reads=[t_bank, t_eps], writes=[t_ms])
                          S.op("act", lambda e: e.activation(out=ms[:], in_=ms[:], func=AF.Exp, scale=-0.5), reads=[t_ms], writes=[t_ms])
                          yat, t_yat = yat_r.next()
                          S.op("dve", lambda e: e.tensor_tensor(out=yat[:], in0=osb[:], in1=ms[:], op=ALU.mult), reads=[t_osb, t_ms], writes=[t_yat])
                          ymix, t_ymix = ymix_r.next()
                          S.op("pool", lambda e: e.tensor_tensor(
                              out=ymix[:, 0:4, :], in0=yat[:].rearrange("p (f t) -> p f t", f=4), in1=sga[:].rearrange("p (f t) -> p f t", f=4), op=ALU.mult),
                              reads=[t_yat, t_sga], writes=[t_ymix])
                          c["ymix"] = (ymix, t_ymix)

                      def b3():
                          state_update(S_f, t_Sf, kx, t_kx, vb, t_vb, cF)
                          S.op("act", lambda e: e.copy(out=Sf_bf[:], in_=S_f[:]), reads=[t_Sf], writes=[t_Sfbf])

                      return [b0, b1, b2, b3]

                  def stageC1(ci, c, cn):
                      z, t_z = c["z"]
                      u, t_u = c["u"] if "u" in c else (None, None)
                      if cn is not None:
                          zn, t_zn = cn["z"]
                          S.op("pool", lambda e: e.tensor_copy(out=z[:, :, 129:130], in_=zn[:, :, 1:2]), reads=[t_zn], writes=[t_z])
                          S.op("pool", lambda e: e.tensor_copy(out=zn[:, :, 0:1], in_=z[:, :, 128:129]), reads=[t_z], writes=[t_zn])
                      else:
                          S.op("pool", lambda e: e.memset(z[:, :, 129:130], 0.0), writes=[t_z])
                      zc, t_zc = zc_r.next()
                      for f in range(4):
                          S.op("dve", lambda e, f=f: e.tensor_scalar(out=zc[:, f, :], in0=z[:, f, 0:128], scalar1=cw[:, f, 0:1], scalar2=None, op0=ALU.mult),
                               reads=[t_z, t_tab], writes=[t_zc])
                          S.op("dve", lambda e, f=f: e.scalar_tensor_tensor(out=zc[:, f, :], in0=z[:, f, 1:129], scalar=cw[:, f, 1:2], in1=zc[:, f, :],
                                                                            op0=ALU.mult, op1=ALU.add), reads=[t_z, t_tab, t_zc], writes=[t_zc])
                          S.op("dve", lambda e, f=f: e.scalar_tensor_tensor(out=zc[:, f, :], in0=z[:, f, 2:130], scalar=cw[:, f, 2:3], in1=zc[:, f, :],
                                                                            op0=ALU.mult, op1=ALU.add), reads=[t_z, t_tab, t_zc], writes=[t_zc])
                      c["zc"] = (zc, t_zc)

                  def stageC2(ci, c):
                      gcp = cb0 + ci
                      ymix, t_ymix = c["ymix"]
                      u, t_u = c["u"]
                      zc, t_zc = c["zc"]
                      S.op("pool", lambda e: e.tensor_tensor(out=ymix[:, 4:8, :], in0=u[:].rearrange("p (f t) -> p f t", f=4), in1=zc[:], op=ALU.mult),
                           reads=[t_u, t_zc], writes=[t_ymix])
                      xres, t_xres = xres_r.next()
                      S.dma("sp", xres[:], x_all[gcp * 128:(gcp + 1) * 128, :], writes=[t_xres])
                      x1t, t_x1t = x1t_r.next()
                      for half in range(2):
                          bank, t_bank = pB.next()
                          cs = slice(half * 512, (half + 1) * 512)
                          for mc in range(8):
                              S.op("pe", lambda e, mc=mc, cs=cs, bank=bank: e.matmul(out=bank[:], lhsT=ymix[:, mc, :], rhs=w_out[:, mc, cs],
                                                                                    start=(mc == 0), stop=(mc == 7)),
                                   reads=[t_ymix, t_wout], writes=[t_bank])
                          S.op("dve", lambda e, cs=cs, bank=bank: e.tensor_tensor(out=x1t[:, cs], in0=bank[:], in1=Gt[:, cs], op=ALU.mult),
                               reads=[t_bank, t_G], writes=[t_x1t])
                      S.op("pool", lambda e: e.tensor_tensor(out=x1t[:], in0=x1t[:], in1=xres[:], op=ALU.add), reads=[t_x1t, t_xres], writes=[t_x1t])
                      S.op("act", lambda e: e.activation(out=junk[:], in_=x1t[:], func=AF.Square, accum_out=ssq1[:, gcp:gcp + 1]),
                           reads=[t_x1t], writes=[t_junk, t_r1[gcp]])
                      rstd_from_ssq(ssq1[:, gcp:gcp + 1], lnt[:, gcp:gcp + 1], rstd1[:, gcp:gcp + 1], float(D), t_r1[gcp])
                      S.dma("pool", x1_hbm[gcp * 128:(gcp + 1) * 128, :], x1t[:], reads=[t_x1t], writes=[t_x1[gcp]], is_output=dbg)

                  fr = {}
                  for k_ in range(min(3, NC)):
                      fr[k_] = frontend(cb0 + k_, grp, False)
                  ctx = {}
                  ctx[0], pa = stageA(0)
                  for p_ in pa:
                      p_()
                  for ci in range(NC):
                      pa = []
                      nrm = None
                      if ci + 3 < NC:
                          nrm = fe_norm(cb0 + ci + 3, *fe_load(cb0 + ci + 3, False))
                      if ci + 1 < NC:
                          ctx[ci + 1], pa = stageA(ci + 1)
                      pb = stageB(ci, ctx[ci])
                      if grp == 1 and ci % NCs == 0:
                          S.op("pool", lambda e: e.memset(S_f[:], 0.0), writes=[t_Sf])
                          S.op("act", lambda e: e.copy(out=Sf_bf[:], in_=S_f[:]), reads=[t_Sf], writes=[t_Sfbf])
                      pb[0]()
                      if pa:
                          pa[0]()
                      pb[1]()
                      if pa:
                          pa[1]()
                      stageC1(ci, ctx[ci], ctx.get(ci + 1) if ci % NCs != NCs - 1 else None)
                      pb[2]()
                      if pa:
                          pa[2]()
                      pb[3]()
                      if grp == 1 and ci % NCs == NCs - 1:
                          S.dma("pool", new_sf[ci // NCs].rearrange("h d e -> d h e"), S_f[:].rearrange("p (h e) -> p h e", h=4),
                                reads=[t_Sf], is_output=True)
                      if pa:
                          pa[3]()
                      if ci + 3 < NC:
                          fr[ci + 3] = fe_T(cb0 + ci + 3, grp, *nrm)
                      stageC2(ci, ctx[ci])
                      ctx.pop(ci - 1, None)
                      fr.pop(ci, None)

              if run_sample:
                  l0_sequence(0, T_S // 128, 0, 1)
              if n_prompts > 0:
                  l0_sequence(T_S // 128, T_P // 128, 1, n_prompts)
              S.barrier()

        if do_l1:
          with contextlib.ExitStack() as s1:
            SCALE = 192.0 ** -0.5
            w1 = sb(s1, "w1", [128, 8, 1856], BF16)
            qup = sb(s1, "qup", [128, 3, 2112], BF16)
            kvup = sb(s1, "kvup", [128, 2, 2048], BF16)
            wo1 = sb(s1, "wo1", [128, 8, 1024], BF16)
            t_w1, t_qup, t_kvup, t_wo1 = T(), T(), T(), T()
            S.op("pool", lambda e: e.memset(w1[:, :, 1792:1856], 0.0), writes=[t_w1])
            S.op("pool", lambda e: e.memset(qup[:, :, 2048:2112], 0.0), writes=[t_qup])
            for c0 in range(0, 1792, 448):
                S.dma("pool", w1[:, :, c0:c0 + 448], w_in1e[:, c0:c0 + 448].rearrange("(c p) n -> p c n", p=128), writes=[t_w1])
            for c0 in range(0, 2048, 512):
                S.dma("pool", qup[:, :, c0:c0 + 512], q_up_e[:, c0:c0 + 512].rearrange("(c p) n -> p c n", p=128), writes=[t_qup])
                S.dma("pool", kvup[:, :, c0:c0 + 512], kv_up_d[:, c0:c0 + 512].rearrange("(c p) n -> p c n", p=128), writes=[t_kvup])
            for c0 in range(0, 1024, 512):
                S.dma("pool", wo1[:, :, c0:c0 + 512], w_out1_d[:, c0:c0 + 512].rearrange("(c p) n -> p c n", p=128), writes=[t_wo1])
            qng = sb(s1, "qng", [128, 3], F32)
            kvng = sb(s1, "kvng", [128, 2], F32)
            rtab = sb(s1, "rtab", [64, 2, 64], F32)
            kvng_bc = sb(s1, "kvng_bc", [128, 256], F32)
            fng_bc = sb(s1, "fng_bc", [128, 1024], F32)
            t_c1 = T()
            S.dma("sp", qng[:], qng_pp, writes=[t_c1])
            S.dma("sp", kvng[:], kvng_pp, writes=[t_c1])
            S.dma("sp", rtab[:], ropetab, writes=[t_c1])
            S.dma("sp", kvng_bc[:], kvng_row[0].partition_broadcast(128), writes=[t_c1])
            S.dma("sp", fng_bc[:], fng_row[0].partition_broadcast(128), writes=[t_c1])
            G1_bc = Ring([sb(s1, "G1_bc%d" % i, [128, 1024], F32) for i in range(2)])
            LKMAX = PAST + T_S
            ckvT = sb(s1, "ckvT", [128, 2, LKMAX], BF16)
            krT = sb(s1, "krT", [128, LKMAX], BF16)
            qlnT = sb(s1, "qlnT", [128, 3, T_S], BF16)
            t_ckvT, t_krT, t_qlnT = T(), T(), T()
            S.op("pool", lambda e: e.memset(krT[64:128, :], 0.0), writes=[t_krT])
            big_r = Ring([sb(s1, "big%d" % i, [128, 8, 512], BF16) for i in range(2)])
            xt1_r = Ring([sb(s1, "xt1_%d" % i, [128, 1024], F32) for i in range(3)])
            junk1 = sb(s1, "junk1", [128, 1024], BF16)
            t_junk1 = T()
            ssq2 = sb(s1, "ssq2", [128, NCH], F32)
            ln2 = sb(s1, "ln2", [128, NCH], F32)
            rstd2 = sb(s1, "rstd2", [128, NCH], F32)
            t_r2 = [T() for _ in range(NCH)]
            t_gsc = [T() for _ in range(NCH)]
            t_osc = {}

            def rope_rotate(src, t_src, srcsw, t_srcsw, dst, t_dst, tok0, ntok, tmp1, t_tmp1, tmp2, t_tmp2):
                r0, nr = tok0 // 64, ntok // 64
                for half in range(2):
                    ps_ = slice(32 * half, 32 * half + 32)
                    if half == 0:
                        cb = rtab[ps_, 0, r0:r0 + nr].unsqueeze(2).broadcast_to([32, nr, 64])
                        sn = rtab[ps_, 1, r0:r0 + nr].unsqueeze(2).broadcast_to([32, nr, 64])
                    else:
                        cb = rtab[ps_, 0, :].unsqueeze(1).broadcast_to([32, nr, 64])
                        sn = rtab[ps_, 1, :].unsqueeze(1).broadcast_to([32, nr, 64])
                    v3 = lambda ap: ap.rearrange("p (r c) -> p r c", c=64)
                    S.op("dve", lambda e, ps_=ps_, cb=cb: e.tensor_tensor(out=v3(tmp1[ps_, 0:ntok]), in0=v3(src[ps_, :]), in1=cb, op=ALU.mult),
                         reads=[t_src, t_c1], writes=[t_tmp1])
                    S.op("dve", lambda e, ps_=ps_, sn=sn: e.tensor_tensor(out=v3(tmp2[ps_, 0:ntok]), in0=v3(srcsw[ps_, :]), in1=sn, op=ALU.mult),
                         reads=[t_srcsw, t_c1], writes=[t_tmp2])
                S.op("pool", lambda e: e.tensor_tensor(out=dst, in0=tmp1[0:64, 0:ntok], in1=tmp2[0:64, 0:ntok], op=ALU.add),
                     reads=[t_tmp1, t_tmp2], writes=[t_dst])

            def silu_to(bank, t_bank, et, t_et, out_ap, t_out):
                S.op("act", lambda e: e.activation(out=et[:], in_=bank[:], func=AF.Exp, scale=-1.0), reads=[t_bank], writes=[t_et])
                S.op("act", lambda e: e.activation(out=et[:], in_=et[:], func=AF.Ln, bias=one_t[:]), reads=[t_et, t_eps], writes=[t_et])
                S.op("act", lambda e: e.activation(out=et[:], in_=et[:], func=AF.Exp, scale=-1.0), reads=[t_et], writes=[t_et])
                S.op("dve", lambda e: e.tensor_tensor(out=out_ap, in0=bank[:].rearrange("p (f t) -> p f t", f=4), in1=et[:].rearrange("p (f t) -> p f t", f=4), op=ALU.mult),
                     reads=[t_bank, t_et], writes=[t_out])

            def l1_sequence(cb0, NC, grp, nseq):
                T_ = NC * 128
                koff = PAST if grp == 0 else 0
                Lk = koff + T_
                NKB = Lk // 128
                QW = 512 if grp == 0 else 256
                NQG = T_ // QW
                row0 = cb0 * 128
                Gt, t_G = G1_bc.next()
                S.dma("sp", Gt[:], g_hbm[1, grp, :].partition_broadcast(128), reads=[t_g], writes=[t_G])
                with contextlib.ExitStack() as sp1:
                    xn_r = Ring([sb(sp1, "xn1_%d" % i, [128, 1024], BF16) for i in range(3)])
                    hT_r = Ring([sb(sp1, "hT1_%d" % i, [128, 8, 128], BF16) for i in range(3)])
                    sq_r = Ring([sb(sp1, "sq1_%d" % i, [128, 384], BF16) for i in range(4)])
                    rs_r = Ring([sb(sp1, "rs1_%d" % i, [128, 128], F32) for i in range(4)])
                    et_r = Ring([sb(sp1, "et1_%d" % i, [128, 512], F32) for i in range(2)])
                    rt1 = sb(sp1, "rt1", [64, 128], F32)
                    rt2 = sb(sp1, "rt2", [64, 128], F32)
                    t_rt1, t_rt2 = T(), T()
                    co_r = Ring([sb(sp1, "co%d" % i, [128, 320], F32) for i in range(2)])
                    sk = sb(sp1, "sk", [128, 4], F32)
                    t_sk = T()
                    cx_r = Ring([sb(sp1, "cx%d" % i, [128, 320], F32) for i in range(2)])
                    cxb_r = Ring([sb(sp1, "cxb%d" % i, [128, 320], BF16) for i in range(2)])
                    if grp == 0:
                        for blk in range(PAST // 128):
                            cx, t_cx = cx_r.next()
                            S.dma("sp", cx[:, 64:320], c_ckv[blk * 128:(blk + 1) * 128, :], writes=[t_cx])
                            S.dma("sp", cx[:, 0:64], c_kr[blk * 128:(blk + 1) * 128, :], writes=[t_cx])
                            cxb, t_cxb = cxb_r.next()
                            S.op("dve", lambda e, cx=cx, cxb=cxb: e.tensor_copy(out=cxb[:], in_=cx[:]), reads=[t_cx], writes=[t_cxb])
                            pt, t_pt = pT.next()
                            for j in range(2):
                                S.op("pe", lambda e, j=j, pt=pt, cxb=cxb: e.transpose(out=pt[:, j, :], in_=cxb[:, 64 + j * 128:64 + (j + 1) * 128], identity=ident_bf[:]),
                                     reads=[t_cxb, t_const], writes=[t_pt])
                            S.op("pe", lambda e, pt=pt, cxb=cxb: e.transpose(out=pt[:, 2, :], in_=cxb[:, 0:128], identity=ident_bf[:]),
                                 reads=[t_cxb, t_const], writes=[t_pt])
                            S.op("act", lambda e, pt=pt, blk=blk: e.copy(out=ckvT[:, :, blk * 128:(blk + 1) * 128], in_=pt[:, 0:2, :]),
                                 reads=[t_pt], writes=[t_ckvT])
                            S.op("act", lambda e, pt=pt, blk=blk: e.copy(out=krT[0:64, blk * 128:(blk + 1) * 128], in_=pt[0:64, 2, :]),
                                 reads=[t_pt], writes=[t_krT])

                    def frontend1(gci):
                        xt, t_xt = xt1_r.next()
                        S.dma("sp", xt[:], x1_hbm[gci * 128:(gci + 1) * 128, :], reads=[t_x1[gci]], writes=[t_xt])
                        xn, t_xn = xn_r.next()
                        S.op("dve", lambda e: e.tensor_scalar(out=xn[:], in0=xt[:], scalar1=rstd1[:, gci:gci + 1], scalar2=None, op0=ALU.mult),
                             reads=[t_xt, t_r1[gci]], writes=[t_xn])
                        pt, t_pt = pT.next()
                        for fc in range(8):
                            S.op("pe", lambda e, fc=fc: e.transpose(out=pt[:, fc, :], in_=xn[:, fc * 128:(fc + 1) * 128], identity=ident_bf[:]),
                                 reads=[t_xn, t_const], writes=[t_pt])
                        hT, t_hT = hT_r.next()
                        for fc in range(8):
                            if fc % 4 == 0:
                                S.op("act", lambda e, fc=fc: e.activation(out=hT[:, fc, :], in_=pt[:, fc, :], func=AF.Identity,
                                                                          scale=A_pp[:, 1, grp, fc:fc + 1], bias=B_pp[:, 1, grp, fc:fc + 1]),
                                     reads=[t_pt, t_AB], writes=[t_hT])
                        for fc in range(8):
                            if fc % 4 != 0:
                                S.op("dve", lambda e, fc=fc: e.tensor_scalar(out=hT[:, fc, :], in0=pt[:, fc, :], scalar1=A_pp[:, 1, grp, fc:fc + 1],
                                                                            scalar2=B_pp[:, 1, grp, fc:fc + 1], op0=ALU.mult, op1=ALU.add),
                                     reads=[t_pt, t_AB], writes=[t_hT])
                        return hT, t_hT

                    def pf(hT, t_hT, cols):
                        bank, t_bank = pB.next()
                        for j, (c0, M) in enumerate(cols):
                            for fc in range(8):
                                S.op("pe", lambda e, fc=fc, j=j, c0=c0, M=M: e.matmul(out=bank[:, j * 128:(j + 1) * 128], lhsT=w1[:, fc, c0:c0 + 128], rhs=hT[:, fc, :],
                                                                                    start=(fc == 0), stop=(fc == 7)), reads=[t_hT, t_w1], writes=[t_bank])
                        return bank, t_bank

                    def feat_rms_a(bank, t_bank, nch):
                        sq, t_sq = sq_r.next()
                        S.op("act", lambda e: e.activation(out=sq[:, 0:nch * 128], in_=bank[:, 0:nch * 128], func=AF.Square), reads=[t_bank], writes=[t_sq])
                        return sq, t_sq

                    def feat_rms_b1(sq, t_sq, nch):
                        b2, t_b2 = pB.next()
                        for j in range(nch):
                            S.op("pe", lambda e, j=j: e.matmul(out=b2[:, 0:128], lhsT=ones_bf[:], rhs=sq[:, j * 128:(j + 1) * 128], start=(j == 0), stop=(j == nch - 1)),
                                 reads=[t_sq, t_const], writes=[t_b2])
                        return b2, t_b2

                    def feat_rms_b2(bank, t_bank, b2, t_b2, nch, n, gtab, dst_fn, t_dst):
                        rs, t_rs = rs_r.next()
                        S.op("act", lambda e: e.activation(out=rs[:, 0:128], in_=b2[:, 0:128], func=AF.Ln, scale=1.0 / n, bias=eps_t[:]), reads=[t_b2, t_eps], writes=[t_rs])
                        S.op("act", lambda e: e.activation(out=rs[:, 0:128], in_=rs[:, 0:128], func=AF.Exp, scale=-0.5), reads=[t_rs], writes=[t_rs])
                        for j in range(nch):
                            S.op("dve", lambda e, j=j: e.scalar_tensor_tensor(out=dst_fn(j), in0=bank[:, j * 128:(j + 1) * 128], scalar=gtab[:, j:j + 1], in1=rs[:, 0:128],
                                                                              op0=ALU.mult, op1=ALU.mult), reads=[t_bank, t_rs, t_c1], writes=[t_dst])

                    fq = [frontend1(cb0)]
                    if nseq * NC > 1:
                        fq.append(frontend1(cb0 + 1))
                    gsl = [None, None]

                    def stream_chunk(si, ci, hT, t_hT, mid_hook):
                        pi = si
                        lc = si * NC + ci
                        gci = cb0 + lc
                        tk = si * Lk + koff + ci * 128
                        gs, t_gs = gsl
                        bankq, t_bankq = pf(hT, t_hT, [(0, 128), (128, 128), (256, 128)])
                        sqq, t_sqq = feat_rms_a(bankq, t_bankq, 3)
                        bankk, t_bankk = pf(hT, t_hT, [(384, 128), (512, 128), (640, 128), (1728, 128)])
                        sqk, t_sqk = feat_rms_a(bankk, t_bankk, 2)
                        mid_hook()
                        b2q, t_b2q = feat_rms_b1(sqq, t_sqq, 3)
                        if lc % 4 == 0:
                            gs, t_gs = big_r.next()
                            gsl[0], gsl[1] = gs, t_gs
                        off = (lc % 4) * 128
                        bankg0, t_bankg0 = pf(hT, t_hT, [(704 + k * 128, 128) for k in range(4)])
                        feat_rms_b2(bankq, t_bankq, b2q, t_b2q, 3, 384.0, qng, lambda j: qlnT[:, j, lc * 128:(lc + 1) * 128], t_qlnT)
                        b2k, t_b2k = feat_rms_b1(sqk, t_sqk, 2)
                        et, t_et = et_r.next()
                        silu_to(bankg0, t_bankg0, et, t_et, gs[:, 0:4, off:off + 128], t_gs)
                        bankg1, t_bankg1 = pf(hT, t_hT, [(704 + 512 + k * 128, 128) for k in range(4)])
                        feat_rms_b2(bankk, t_bankk, b2k, t_b2k, 2, 256.0, kvng, lambda j: ckvT[:, j, tk:tk + 128], t_ckvT)
                        if grp == 0:
                            rope_rotate(bankk[0:64, 256:384], t_bankk, bankk[0:64, 384:512], t_bankk, krT[0:64, tk:tk + 128], t_krT,
                                        ci * 128, 128, rt1, t_rt1, rt2, t_rt2)
                        else:
                            S.op("act", lambda e: e.copy(out=krT[0:64, tk:tk + 128], in_=bankk[0:64, 256:384]), reads=[t_bankk], writes=[t_krT])
                        et, t_et = et_r.next()
                        silu_to(bankg1, t_bankg1, et, t_et, gs[:, 4:8, off:off + 128], t_gs)
                        if lc % 4 == 3 or lc == nseq * NC - 1:
                            w = (lc % 4 + 1) * 128
                            r0 = row0 + (lc // 4) * 512
                            S.dma("pool", gsc[:, :, r0:r0 + w].rearrange("h p t -> p h t"), gs[:, :, 0:w], reads=[t_gs], writes=[t_gsc[(r0 // 512)]])
                        if grp == 1:
                            bank, t_bank = pB.next()
                            for fc in range(8):
                                S.op("pe", lambda e, fc=fc, bank=bank: e.matmul(out=bank[:, 0:320], lhsT=hT[:, fc, :], rhs=w1[:, fc, 384:704], start=(fc == 0), stop=(fc == 7)),
                                     reads=[t_hT, t_w1], writes=[t_bank])
                            co, t_co = co_r.next()
                            S.op("act", lambda e, bank=bank: e.activation(out=junk1[:, 0:256], in_=bank[:, 0:256], func=AF.Square, accum_out=sk[:, 0:1]),
                                 reads=[t_bank], writes=[t_junk1, t_sk])
                            S.op("act", lambda e: e.activation(out=sk[:, 1:2], in_=sk[:, 0:1], func=AF.Ln, scale=1.0 / 256, bias=eps_t[:]), reads=[t_sk, t_eps], writes=[t_sk])
                            S.op("act", lambda e: e.activation(out=sk[:, 2:3], in_=sk[:, 1:2], func=AF.Exp, scale=-0.5), reads=[t_sk], writes=[t_sk])
                            S.op("dve", lambda e, bank=bank, co=co: e.scalar_tensor_tensor(out=co[:, 0:256], in0=bank[:, 0:256], scalar=sk[:, 2:3], in1=kvng_bc[:],
                                                                                           op0=ALU.mult, op1=ALU.mult), reads=[t_bank, t_sk, t_c1], writes=[t_co])
                            S.op("act", lambda e, bank=bank, co=co: e.copy(out=co[:, 256:320], in_=bank[:, 256:320]), reads=[t_bank], writes=[t_co])
                            S.dma("pool", new_ckv[pi, ci * 128:(ci + 1) * 128, :], co[:, 0:256], reads=[t_co], is_output=True)
                            S.dma("pool", new_kr[pi, ci * 128:(ci + 1) * 128, :], co[:, 256:320], reads=[t_co], is_output=True)

                    for lc_ in range(nseq * NC):
                        hT, t_hT = fq.pop(0)

                        def hook(lc_=lc_):
                            if lc_ + 2 < nseq * NC:
                                fq.append(frontend1(cb0 + lc_ + 2))
                        stream_chunk(lc_ // NC, lc_ % NC, hT, t_hT, hook)
                    S.barrier()
                with contextlib.ExitStack() as sp2:
                    NKV = 1 if grp == 0 else 2
                    kT_ring = Ring([sb(sp2, "kTh%d" % i, [128, Lk], BF16) for i in range(NKV)])
                    V_ring = Ring([sb(sp2, "Vh%d" % i, [128, Lk // 128, 128], BF16) for i in range(NKV)])
                    qT_r = Ring([sb(sp2, "qTg%d" % i, [128, 512], BF16) for i in range(2)])
                    qr_r = Ring([sb(sp2, "qrg%d" % i, [128, 512], BF16) for i in range(2)])
                    for _b, _t in zip(qr_r.bufs, qr_r.ts):
                        S.op("pool", lambda e, _b=_b: e.memset(_b[64:128, :], 0.0), writes=[_t])
                    PT_r = Ring([sb(sp2, "PT%d" % i, [128, 512], BF16) for i in range(4)])
                    rs_r = Ring([sb(sp2, "rsa%d" % i, [128, 512], F32) for i in range(1)])
                    tt_r = Ring([sb(sp2, "tta%d" % i, [128, 512], F32) for i in range(1)])
                    gt_r = Ring([sb(sp2, "gt%d" % i, [128, 512], BF16) for i in range(2)])
                    ot_r = Ring([sb(sp2, "ot%d" % i, [128, 512], BF16) for i in range(2)])
                    accD_r = Ring([sb(sp2, "accD%d" % i, [128, 512], F32) for i in range(2)])
                    SUM_MOD = 4
                    POOL_SUMS = False
                    accP_r = Ring([sb(sp2, "accP%d" % i, [128, 512], F32) for i in range(1)])
                    ra1 = sb(sp2, "ra1", [64, 512], F32)
                    ra2 = sb(sp2, "ra2", [64, 512], F32)
                    t_ra1, t_ra2 = T(), T()
                    ST_AHEAD = 2
                    stR = pB.sub([0, 1, 2])
                    accR = pB.sub([3, 4, 5])
                    expR = Ring([pT.bufs[i][:].rearrange("p a b -> p (a b)").bitcast(F32) for i in range(2)])
                    expR.ts = [pT.ts[i] for i in range(2)]
                    def expand_kv(si, h):
                        kTh, t_kTh = kT_ring.next()
                        Vh, t_Vh = V_ring.next()
                        kb_ = si * Lk
                        for k0 in range(0, Lk, 512):
                            kw = min(512, Lk - k0)
                            bank, t_bank = expR.next()
                            for j in range(2):
                                S.op("pe", lambda e, j=j, bank=bank, k0=k0, kw=kw: e.matmul(out=bank[:, 0:kw], lhsT=kvup[:, j, h * 256:h * 256 + 128], rhs=ckvT[:, j, kb_ + k0:kb_ + k0 + kw],
                                                                                         start=(j == 0), stop=(j == 1)), reads=[t_kvup, t_ckvT], writes=[t_bank])
                            S.op("act", lambda e, bank=bank, k0=k0, kw=kw: e.copy(out=kTh[:, k0:k0 + kw], in_=bank[:, 0:kw]), reads=[t_bank], writes=[t_kTh])
                        for kb0 in range(0, NKB, 4):
                            nb = min(4, NKB - kb0)
                            bank, t_bank = expR.next()
                            for b in range(nb):
                                kb = kb0 + b
                                for j in range(2):
                                    S.op("pe", lambda e, j=j, bank=bank, kb=kb, b=b: e.matmul(out=bank[:, b * 128:(b + 1) * 128], lhsT=ckvT[:, j, kb_ + kb * 128:kb_ + (kb + 1) * 128],
                                                                                           rhs=kvup[:, j, h * 256 + 128:h * 256 + 256], start=(j == 0), stop=(j == 1)),
                                         reads=[t_kvup, t_ckvT], writes=[t_bank])
                            S.op("dve", lambda e, bank=bank, kb0=kb0, nb=nb: e.tensor_copy(out=Vh[:, kb0:kb0 + nb, :], in_=bank[:, 0:nb * 128].rearrange("p (b e) -> p b e", e=128)),
                                 reads=[t_bank], writes=[t_Vh])
                        return (kTh, t_kTh, Vh, t_Vh)

                    def expand_q(si, h, qg):
                        q0 = si * T_ + qg * QW
                        rows = slice(row0 + q0, row0 + q0 + QW)
                        gt, t_gt = gt_r.next()
                        S.dma("sp", gt[:, 0:QW], gsc[h, :, rows], reads=[t_gsc[(row0 + q0) // 512]], writes=[t_gt])
                        bank, t_bank = expR.next()
                        for j in range(3):
                            S.op("pe", lambda e, j=j: e.matmul(out=bank[:, 0:QW], lhsT=qup[:, j, h * 256:h * 256 + 128], rhs=qlnT[:, j, q0:q0 + QW],
                                                              start=(j == 0), stop=(j == 2)), reads=[t_qup, t_qlnT], writes=[t_bank])
                        qTg, t_qTg = qT_r.next()
                        S.op("act", lambda e: e.copy(out=qTg[:, 0:QW], in_=bank[:, 0:QW]), reads=[t_bank], writes=[t_qTg])
                        bank_r, t_bank_r = expR.next()
                        for j in range(3):
                            S.op("pe", lambda e, j=j: e.matmul(out=bank_r[:, 0:QW], lhsT=qup[:, j, h * 256 + 128:h * 256 + 256], rhs=qlnT[:, j, q0:q0 + QW],
                                                              start=(j == 0), stop=(j == 2)), reads=[t_qup, t_qlnT], writes=[t_bank_r])
                        qrg, t_qrg = qr_r.next()
                        if grp == 0:
                            bank_s, t_bank_s = expR.next()
                            for j in range(3):
                                S.op("pe", lambda e, j=j: e.matmul(out=bank_s[:, 0:QW], lhsT=qup[:, j, h * 256 + 192:h * 256 + 320], rhs=qlnT[:, j, q0:q0 + QW],
                                                                  start=(j == 0), stop=(j == 2)), reads=[t_qup, t_qlnT], writes=[t_bank_s])
                            rope_rotate(bank_r[0:64, 0:QW], t_bank_r, bank_s[0:64, 0:QW], t_bank_s, qrg[0:64, 0:QW], t_qrg, qg * QW, QW, ra1, t_ra1, ra2, t_ra2)
                        else:
                            S.op("act", lambda e: e.copy(out=qrg[0:64, 0:QW], in_=bank_r[0:64, 0:QW]), reads=[t_bank_r], writes=[t_qrg])
                        return (qTg, t_qTg, qrg, t_qrg, gt, t_gt)

                    def do_qg(si, h, qg, kvb, pre, nxt_item):
                        q0 = si * T_ + qg * QW
                        rows = slice(row0 + q0, row0 + q0 + QW)
                        qTg, t_qTg, qrg, t_qrg, gt, t_gt = pre
                        kTh, t_kTh, Vh, t_Vh = kvb
                        kb_ = si * Lk
                        Ob, t_Ob = accR.next()
                        Sb, t_Sb = accR.next()
                        dve_blocks = [kb for kb in range(NKB) if kb % SUM_MOD != SUM_MOD - 1]
                        pe_blocks = [kb for kb in range(NKB) if kb % SUM_MOD == SUM_MOD - 1]
                        accD, t_accD = accD_r.next()
                        accP, t_accP = accP_r.next()
                        res = [None]

                        def emit_st(kb):
                            ks = slice(kb * 128, (kb + 1) * 128)
                            stb, t_stb = stR.next()
                            S.op("pe", lambda e: e.matmul(out=stb[:, 0:QW], lhsT=kTh[:, ks], rhs=qTg[:, 0:QW], start=True, stop=False),
                                 reads=[t_kTh, t_qTg], writes=[t_stb])
                            ks2 = slice(kb_ + kb * 128, kb_ + (kb + 1) * 128)
                            S.op("pe", lambda e: e.matmul(out=stb[:, 0:QW], lhsT=krT[:, ks2], rhs=qrg[:, 0:QW], start=False, stop=True),
                                 reads=[t_krT, t_qrg], writes=[t_stb])
                            PT, t_PT = PT_r.next()
                            S.op("act", lambda e: e.activation(out=PT[:, 0:QW], in_=stb[:, 0:QW], func=AF.Exp, scale=SCALE), reads=[t_stb], writes=[t_PT])
                            if POOL_SUMS and kb in pe_blocks:
                                if kb == pe_blocks[0]:
                                    S.op("pool", lambda e: e.tensor_copy(out=accP[:, 0:QW], in_=PT[:, 0:QW]), reads=[t_PT], writes=[t_accP])
                                else:
                                    S.op("pool", lambda e: e.tensor_tensor(out=accP[:, 0:QW], in0=accP[:, 0:QW], in1=PT[:, 0:QW], op=ALU.add),
                                         reads=[t_PT, t_accP], writes=[t_accP])
                            if kb in dve_blocks:
                                if kb == dve_blocks[0]:
                                    S.op("dve", lambda e: e.tensor_copy(out=accD[:, 0:QW], in_=PT[:, 0:QW]), reads=[t_PT], writes=[t_accD])
                                else:
                                    S.op("dve", lambda e: e.tensor_tensor(out=accD[:, 0:QW], in0=accD[:, 0:QW], in1=PT[:, 0:QW], op=ALU.add),
                                         reads=[t_PT, t_accD], writes=[t_accD])
                            return PT, t_PT

                        def emit_pv(kb, PT, t_PT):
                            S.op("pe", lambda e: e.matmul(out=Ob[:, 0:QW], lhsT=Vh[:, kb, :], rhs=PT[:, 0:QW], start=(kb == 0), stop=(kb == NKB - 1)),
                                 reads=[t_Vh, t_PT], writes=[t_Ob])
                            if kb in pe_blocks and not POOL_SUMS:
                                S.op("pe", lambda e: e.matmul(out=Sb[:, 0:QW], lhsT=ones_bf[:], rhs=PT[:, 0:QW], start=(kb == pe_blocks[0]), stop=False),
                                     reads=[t_const, t_PT], writes=[t_Sb])

                        pend = [emit_st(kb) for kb in range(min(ST_AHEAD, NKB))]
                        for kb in range(NKB):
                            if kb + ST_AHEAD < NKB:
                                pend.append(emit_st(kb + ST_AHEAD))
                            emit_pv(kb, *pend.pop(0))
                            if kb == (NKB * 2) // 3 and nxt_item is not None:
                                if (nxt_item[0], nxt_item[1]) == (si, h):
                                    res[0] = (kvb, expand_q(*nxt_item))
                                elif NKV == 2:
                                    res[0] = (expand_kv(nxt_item[0], nxt_item[1]), expand_q(*nxt_item))
                        if POOL_SUMS and pe_blocks:
                            S.op("pe", lambda e: e.matmul(out=Sb[:, 0:QW], lhsT=ones_f32[:], rhs=accP[:, 0:QW], start=True, stop=False),
                                 reads=[t_const, t_accP], writes=[t_Sb])
                        S.op("pe", lambda e: e.matmul(out=Sb[:, 0:QW], lhsT=ones_f32[:], rhs=accD[:, 0:QW], start=(len(pe_blocks) == 0), stop=True),
                             reads=[t_const, t_accD], writes=[t_Sb])
                        rs, t_rs = rs_r.next()
                        S.op("act", lambda e: e.activation(out=rs[:, 0:QW], in_=Sb[:, 0:QW], func=AF.Ln), reads=[t_Sb], writes=[t_rs])
                        S.op("act", lambda e: e.activation(out=rs[:, 0:QW], in_=rs[:, 0:QW], func=AF.Exp, scale=-1.0), reads=[t_rs], writes=[t_rs])
                        tt, t_tt = tt_r.next()
                        S.op("dve", lambda e: e.tensor_tensor(out=tt[:, 0:QW], in0=Ob[:, 0:QW], in1=rs[:, 0:QW], op=ALU.mult), reads=[t_Ob, t_rs], writes=[t_tt])
                        ot, t_ot = ot_r.next()
                        S.op("pool", lambda e: e.tensor_tensor(out=ot[:, 0:QW], in0=tt[:, 0:QW], in1=gt[:, 0:QW], op=ALU.mult), reads=[t_tt, t_gt], writes=[t_ot])
                        key = (h, (row0 + q0) // 256)
                        t_osc[key] = T()
                        S.dma("pool", osc[h, :, rows], ot[:, 0:QW], reads=[t_ot], writes=[t_osc[key]])
                        return res[0]

                    items = [(si, h, qg) for si in range(nseq) for h in range(8) for qg in range(NQG)]
                    pre = None
                    for i, (si, h, qg) in enumerate(items):
                        if pre is None:
                            pre = (expand_kv(si, h), expand_q(si, h, qg))
                        nxt_item = items[i + 1] if i + 1 < len(items) else None
                        pre = do_qg(si, h, qg, pre[0], pre[1], nxt_item)
                    S.barrier()
                with contextlib.ExitStack() as sp3:
                    x2_r = Ring([sb(sp3, "x2_%d" % i, [128, 1024], F32) for i in range(4)])
                    yo_r = Ring([sb(sp3, "yo_%d" % i, [128, 1024], F32) for i in range(3)])
                    xr_r = Ring([sb(sp3, "xr_%d" % i, [128, 1024], F32) for i in range(4)])
                    GWc = QW // 128
                    def out_group(g0):
                        r0 = row0 + g0 * 128
                        ogt, t_ogt = big_r.next()
                        deps = [t_osc[(h, r0 // 256)] for h in range(8)]
                        S.dma("sp", ogt[:, :, 0:QW], osc[:, :, r0:r0 + QW].rearrange("h p t -> p h t"), reads=deps, writes=[t_ogt])
                        for c in range(GWc):
                            st_ = out_chunk(g0, c, ogt, t_ogt)
                            if pend_out:
                                out_finish(*pend_out.pop(0))
                            pend_out.append(st_)

                    def out_chunk(g0, c, ogt, t_ogt):
                        if True:
                            gci = cb0 + g0 + c
                            xres, t_xres = xr_r.next()
                            S.dma("sp", xres[:], x1_hbm[gci * 128:(gci + 1) * 128, :], reads=[t_x1[gci]], writes=[t_xres])
                            x2, t_x2 = x2_r.next()
                            for half in range(2):
                                bank, t_bank = pB.next()
                                cs = slice(half * 512, (half + 1) * 512)
                                for hh in range(8):
                                    S.op("pe", lambda e, hh=hh, cs=cs, bank=bank, c=c: e.matmul(out=bank[:], lhsT=ogt[:, hh, c * 128:(c + 1) * 128], rhs=wo1[:, hh, cs],
                                                                                              start=(hh == 0), stop=(hh == 7)), reads=[t_ogt, t_wo1], writes=[t_bank])
                                S.op("dve", lambda e, cs=cs, bank=bank, x2=x2: e.tensor_tensor(out=x2[:, cs], in0=bank[:], in1=Gt[:, cs], op=ALU.mult),
                                     reads=[t_bank, t_G], writes=[t_x2])
                            S.op("pool", lambda e, x2=x2, xres=xres: e.tensor_tensor(out=x2[:], in0=x2[:], in1=xres[:], op=ALU.add), reads=[t_x2, t_xres], writes=[t_x2])
                            S.op("act", lambda e, x2=x2, gci=gci: e.activation(out=junk1[:], in_=x2[:], func=AF.Square, accum_out=ssq2[:, gci:gci + 1]),
                                 reads=[t_x2], writes=[t_junk1, t_r2[gci]])
                            S.op("act", lambda e, gci=gci: e.activation(out=ln2[:, gci:gci + 1], in_=ssq2[:, gci:gci + 1], func=AF.Ln, scale=1.0 / D, bias=eps_t[:]),
                                 reads=[t_r2[gci], t_eps], writes=[t_r2[gci]])
                            S.op("act", lambda e, gci=gci: e.activation(out=rstd2[:, gci:gci + 1], in_=ln2[:, gci:gci + 1], func=AF.Exp, scale=-0.5),
                                 reads=[t_r2[gci]], writes=[t_r2[gci]])
                            return (gci, x2, t_x2)

                    def out_finish(gci, x2, t_x2):
                        if True:
                            yo, t_yo = yo_r.next()
                            S.op("dve", lambda e, x2=x2, yo=yo, gci=gci: e.scalar_tensor_tensor(out=yo[:], in0=x2[:], scalar=rstd2[:, gci:gci + 1], in1=fng_bc[:],
                                                                                               op0=ALU.mult, op1=ALU.mult), reads=[t_x2, t_r2[gci], t_c1], writes=[t_yo])
                            S.dma("act", y_all[gci * 128:(gci + 1) * 128, :], yo[:], reads=[t_yo], is_output=True)

                    pend_out = []
                    for g0 in range(0, nseq * NC, GWc):
                        out_group(g0)
                    while pend_out:
                        out_finish(*pend_out.pop(0))
                    S.barrier()

            if run_sample:
                l1_sequence(0, T_S // 128, 0, 1)
            if n_prompts > 0:
                l1_sequence(T_S // 128, T_P // 128, 1, n_prompts)
            S.barrier()

        S.finish()
        block = st.enter_context(nc.Block())
        S.emit(block)
    return nc


def prep_inputs(inp, cores=range(N_CORES)):
    f = lambda a: np.ascontiguousarray(np.asarray(a, dtype=np.float32))
    tab, _, _ = l0_tables()
    shared = {
        "ada_w": f(inp["ada_w"]),
        "ada_b_pp": f(np.asarray(inp["ada_b"]).reshape(2, 24, 128).transpose(2, 0, 1)),
        "ada_b_row": f(inp["ada_b"]),
        "ng_pp": f(np.asarray(inp["norm_g"]).reshape(2, 8, 128).transpose(2, 0, 1)),
        "w_in0": f(inp["even_in_w"][0]),
        "convw_pp": f(np.asarray(inp["even_conv_w"])[0].reshape(3, 4, 128).transpose(2, 1, 0)),
        "w_out0": f(inp["even_out_w"][0]),
        "ident": np.eye(128, dtype=np.float32),
        "l0tab": tab,
    }
    perm = np.concatenate([np.arange(16, 32), np.arange(0, 16), np.arange(48, 64), np.arange(32, 48)])
    w1 = np.asarray(inp["odd_in_w"])[0]
    shared["w_in1e"] = f(np.concatenate([w1, w1[:, 640 + perm]], axis=1))
    qu = np.asarray(inp["odd_q_up_w"])[0]
    parts = []
    for h in range(8):
        b = h * 192
        parts += [qu[:, b:b + 128], qu[:, b + 128:b + 192], qu[:, b + 128 + perm]]
    shared["q_up_e"] = f(np.concatenate(parts, axis=1))
    shared["kv_up"] = f(np.asarray(inp["odd_kv_up_w"])[0])
    shared["w_out1"] = f(np.asarray(inp["odd_out_w"])[0])
    shared["qng_pp"] = f(np.asarray(inp["odd_q_norm_g"])[0].reshape(3, 128).T)
    shared["kvng_pp"] = f(np.asarray(inp["odd_kv_norm_g"])[0].reshape(2, 128).T)
    shared["kvng_row"] = f(np.asarray(inp["odd_kv_norm_g"])[0].reshape(1, 256))
    shared["fng_row"] = f(np.asarray(inp["final_norm_g"]).reshape(1, D))
    shared["ropetab"] = rope_table()
    maps = []
    xs = np.asarray(inp["x_sample"])
    xp = np.asarray(inp["x_prompt"])
    c = np.asarray(inp["c"])
    cc = np.asarray(inp["c_ctx"])
    for r in cores:
        m = dict(shared)
        m["x_all"] = f(np.concatenate([xs[r], xp[N_P * r:N_P * (r + 1)].reshape(N_P * T_P, D)], axis=0))
        m["cvec"] = f(np.stack([c[r].reshape(8, 128).T, cc.reshape(8, 128).T], axis=-1))
        m["st_f"] = f(inp["state_ret_fwd"][r, 0])
        m["st_b"] = f(inp["state_ret_bwd"][r, 0])
        m["c_ckv"] = f(inp["cache_mla_ckv"][r, 0])
        m["c_kr"] = f(inp["cache_mla_krope"][r, 0])
        maps.append(m)
    return maps


_NC_CACHE = {}


def kernel(**inputs):
    if "nc" not in _NC_CACHE:
        _NC_CACHE["nc"] = build_program()
    nc = _NC_CACHE["nc"]
    maps = prep_inputs(inputs)
    res = run_bass_kernel_spmd(nc, maps, core_ids=list(range(N_CORES)))
    outs = res.results
    y_prompt = np.zeros((N_CORES * N_P, T_P, D), np.float32)
    y_sample = np.zeros((N_CORES, T_S, D), np.float32)
    nsf = np.zeros((N_CORES * N_P, 1, 4, 128, 128), np.float32)
    nsb = np.zeros((N_CORES * N_P, 1, 4, 128, 128), np.float32)
    nckv = np.zeros((N_CORES * N_P, 1, T_P, 256), np.float32)
    nkr = np.zeros((N_CORES * N_P, 1, T_P, 64), np.float32)
    for r in range(N_CORES):
        o = outs[r]
        y_sample[r] = o["y_all"][:T_S]
        y_prompt[N_P * r:N_P * (r + 1)] = o["y_all"][T_S:].reshape(N_P, T_P, D)
        nsf[N_P * r:N_P * (r + 1), 0] = o["new_sf"]
        nsb[N_P * r:N_P * (r + 1), 0] = o["new_sb"]
        nckv[N_P * r:N_P * (r + 1), 0] = o["new_ckv"]
        nkr[N_P * r:N_P * (r + 1), 0] = o["new_kr"]
    return (y_prompt, y_sample, nsf, nsb, nckv, nkr)
```
